# Optimizing a Trainium2 kernel written in Bass

```python
import jax, jax.numpy as jnp
from jax import lax
import numpy as np

D_MODEL = 1024
BATCH = 16
SEQ = 2048
DEPTH = 2

CHUNK = 64
D_FF = 2816
CONV_W = D_MODEL
CONV_K = 3
CONV_GROUPS = 16
POOL_W = D_MODEL
POOL_WINDOWS = (2, 4, 8, 16)
POOL_GROUPS = len(POOL_WINDOWS)
POOL_GC = POOL_W // POOL_GROUPS
N_BRANCH = 2
IN_COLS = 3 * CONV_W + POOL_W + N_BRANCH * D_MODEL
EPS = 1e-6

kernel_name = "hybrid_conv_pool_gated_macaron"


def rms_norm(x, g):
    xf = x.astype(jnp.float32)
    y = xf * lax.rsqrt(jnp.mean(xf * xf, axis=-1, keepdims=True) + EPS)
    return (y * g.astype(jnp.float32)).astype(x.dtype)


def swiglu(h, w_gate_up, w_down):
    gu = h @ w_gate_up
    g, u = jnp.split(gu, 2, axis=-1)
    return (jax.nn.silu(g) * u) @ w_down


def causal_depthwise_conv(u, w):
    s = u.shape[1]
    u_pad = jnp.pad(u, ((0, 0), (CONV_K - 1, 0), (0, 0)))
    y = w[0] * u_pad[:, 0:s]
    for k in range(1, CONV_K):
        y = y + w[k] * u_pad[:, k:k + s]
    return y


def trailing_mean(u, win):
    s = u.shape[1]
    cs = jnp.cumsum(u, axis=1)
    cs_shift = jnp.pad(cs, ((0, 0), (win, 0), (0, 0)))[:, :s]
    count = jnp.minimum(jnp.arange(1, s + 1), win).astype(jnp.float32)
    return (cs - cs_shift) / count[None, :, None]


def pool_mixer(p, w_pg, b_pg, scale):
    b, s, _ = p.shape
    pf = p.astype(jnp.float32).reshape(b, s, POOL_GROUPS, POOL_GC)
    pooled = jnp.stack([trailing_mean(pf[:, :, gi], win) for gi, win in enumerate(POOL_WINDOWS)], axis=2)
    d = (pooled - pf).astype(p.dtype)
    y = jnp.einsum('bsgc,gcd->bsgd', d, w_pg) + b_pg
    return y.reshape(b, s, POOL_W) * scale


def hybrid_mixer(h, w_in, b_gate, conv_w, w_conv_out, w_pg, b_pg, pool_scale, w_pool_out, w_o):
    z = h @ w_in
    b_g, c_g, v, p, gates = jnp.split(
        z, [CONV_W, 2 * CONV_W, 3 * CONV_W, 3 * CONV_W + POOL_W], axis=-1)
    y_a = (b_g * causal_depthwise_conv(c_g * v, conv_w)) @ w_conv_out
    y_p = pool_mixer(p, w_pg, b_pg, pool_scale) @ w_pool_out
    g = jax.nn.sigmoid((gates + b_gate).astype(jnp.float32)).astype(h.dtype)
    g_a, g_p = jnp.split(g, N_BRANCH, axis=-1)
    return (g_a * y_a + g_p * y_p) @ w_o


def setup_inputs(seed: int = 0) -> dict:
    key = jax.random.key(seed)
    ks = jax.random.split(key, 24)
    f32 = jnp.float32

    def nrm(k, shape, fan_in):
        return jax.random.normal(k, shape, f32) * (fan_in ** -0.5)

    def gain(k, shape):
        return 1.0 + 0.05 * jax.random.normal(k, shape, f32)

    L = DEPTH
    return {
        "x": jax.random.normal(ks[0], (BATCH, SEQ, D_MODEL), f32),
        "ffn1_pre": gain(ks[1], (L, D_MODEL)),
        "ffn1_post": gain(ks[2], (L, D_MODEL)),
        "ffn1_w_gate_up": nrm(ks[3], (L, D_MODEL, 2 * D_FF), D_MODEL),
        "ffn1_w_down": nrm(ks[4], (L, D_FF, D_MODEL), D_FF),
        "mix_pre": gain(ks[5], (L, D_MODEL)),
        "mix_post": gain(ks[6], (L, D_MODEL)),
        "w_in": nrm(ks[7], (L, D_MODEL, IN_COLS), D_MODEL),
        "b_gate": 0.01 * jax.random.normal(ks[8], (L, N_BRANCH * D_MODEL), f32),
        "conv_w": nrm(ks[9], (L, CONV_K, CONV_W), CONV_K),
        "w_conv_out": nrm(ks[10], (L, CONV_W, D_MODEL), CONV_W),
        "w_pool_group": nrm(ks[11], (L, POOL_GROUPS, POOL_GC, POOL_GC), POOL_GC),
        "b_pool_group": 0.01 * jax.random.normal(ks[12], (L, POOL_GROUPS, POOL_GC), f32),
        "pool_scale": gain(ks[13], (L, POOL_W)),
        "w_pool_out": nrm(ks[14], (L, POOL_W, D_MODEL), POOL_W),
        "w_o": nrm(ks[15], (L, D_MODEL, D_MODEL), D_MODEL),
        "ffn2_pre": gain(ks[16], (L, D_MODEL)),
        "ffn2_post": gain(ks[17], (L, D_MODEL)),
        "ffn2_w_gate_up": nrm(ks[18], (L, D_MODEL, 2 * D_FF), D_MODEL),
        "ffn2_w_down": nrm(ks[19], (L, D_FF, D_MODEL), D_FF),
    }


def reference(x, ffn1_pre, ffn1_post, ffn1_w_gate_up, ffn1_w_down,
              mix_pre, mix_post, w_in, b_gate, conv_w, w_conv_out,
              w_pool_group, b_pool_group, pool_scale, w_pool_out, w_o,
              ffn2_pre, ffn2_post, ffn2_w_gate_up, ffn2_w_down):
    for l in range(DEPTH):
        h = swiglu(rms_norm(x, ffn1_pre[l]), ffn1_w_gate_up[l], ffn1_w_down[l])
        x = x + 0.5 * rms_norm(h, ffn1_post[l])
        h = hybrid_mixer(rms_norm(x, mix_pre[l]), w_in[l], b_gate[l], conv_w[l],
                         w_conv_out[l], w_pool_group[l], b_pool_group[l],
                         pool_scale[l], w_pool_out[l], w_o[l])
        x = x + rms_norm(h, mix_post[l])
        h = swiglu(rms_norm(x, ffn2_pre[l]), ffn2_w_gate_up[l], ffn2_w_down[l])
        x = x + 0.5 * rms_norm(h, ffn2_post[l])
    return x
```

```python
import bisect
from contextlib import ExitStack

import numpy as np
import concourse.bass as bass
import concourse.mybir as mybir
from concourse.bass_utils import run_bass_kernel_spmd

F32 = mybir.dt.float32
BF16 = mybir.dt.bfloat16
AF = mybir.ActivationFunctionType
ALU = mybir.AluOpType

D = 1024
KC = 8
DFF = 2816
FC = 22
EPS = 1e-6
NCORES = 8
POOL_WINDOWS = (2, 4, 8, 16)
RSTD_MODE = "lnexp"

ENGS = ("pe", "act", "dve", "pool", "sp")

SM_F1PRE, SM_F1POST, SM_MPRE, SM_MPOST, SM_F2PRE, SM_F2POST = 0, 8, 16, 24, 32, 40
SM_BG = 48
SM_CW = 64
SM_BPG = 88
SM_PS = 96
SM_PER_LAYER = 104


class Prog:
    def __init__(self):
        self.ins = {e: [] for e in ENGS}
        self.res = {}
        self.seen = {e: {} for e in ENGS}
        self.scount = {}
        self.milestones = {e: set() for e in ENGS}

    def emit(self, eng, fn, reads=(), writes=(), stream=None):
        idx = len(self.ins[eng])
        if stream is None:
            me = (eng, idx)
        else:
            sidx = self.scount.get(stream, 0)
            self.scount[stream] = sidx + 1
            me = (stream, sidx)
        deps = {}

        def add(dep, kind):
            if dep is None:
                return
            src, i = dep
            if stream is None and src == eng:
                if eng == "pe" or kind == "war":
                    return
            if i > deps.get(src, -1):
                deps[src] = i

        for r in reads:
            st = self.res.get(r)
            if st is not None:
                add(st["w"], "raw")
        for w in writes:
            st = self.res.get(w)
            if st is not None:
                add(st["w"], "waw")
                for rd in st["r"]:
                    add(rd, "war")
        waits = []
        for src, i in deps.items():
            if i > self.seen[eng].get(src, -1):
                self.seen[eng][src] = i
                waits.append((src, i))
                if src in self.milestones:
                    self.milestones[src].add(i)
        for r in reads:
            st = self.res.setdefault(r, {"w": None, "r": []})
            st["r"].append(me)
        for w in writes:
            self.res[w] = {"w": me, "r": []}
        self.ins[eng].append((waits, fn, stream, idx))
        return me

    def replay(self, eng, e, sems):
        ms = sorted(self.milestones[eng])
        msorted = {s: sorted(self.milestones[s]) for s in self.milestones}

        def semval(src, i):
            if src in msorted:
                return bisect.bisect_right(msorted[src], i)
            return 16 * (i + 1)

        msset = set(ms)
        for waits, fn, stream, idx in self.ins[eng]:
            for src, i in waits:
                e.wait_ge(sems[src], semval(src, i))
            if fn is None:
                continue
            inst = fn(e)
            if stream is not None:
                inst.then_inc(sems[stream], 16)
            elif idx in msset:
                inst.then_inc(sems[eng], 1)


def build_program(nseq, seq, depth, tt=1024, nslots=4, ntmp=11):
    NT = tt // 512
    assert tt % 512 == 0 and seq % tt == 0
    ntok = nseq * seq
    nsub = tt // 128
    nc = bass.Bass("TRN2", target_bir_lowering=False)

    x_d = nc.dram_tensor("x", [ntok, D], F32, kind="ExternalInput").ap()
    out_d = nc.dram_tensor("out", [ntok, D], F32, kind="ExternalOutput").ap()
    sm_d = nc.dram_tensor("smalls", [128, SM_PER_LAYER * depth], F32, kind="ExternalInput").ap()
    id_d = nc.dram_tensor("ident", [128, 128], F32, kind="ExternalInput").ap()
    W = []
    for l in range(depth):
        d = {}
        d["f1gu"] = nc.dram_tensor(f"f1gu{l}", [D, 2 * DFF], F32, kind="ExternalInput").ap()
        d["f1d"] = nc.dram_tensor(f"f1d{l}", [DFF, D], F32, kind="ExternalInput").ap()
        d["win"] = nc.dram_tensor(f"win{l}", [D, 6 * D], F32, kind="ExternalInput").ap()
        d["wco"] = nc.dram_tensor(f"wco{l}", [D, D], F32, kind="ExternalInput").ap()
        d["wpg"] = nc.dram_tensor(f"wpg{l}", [4 * 256, 256], F32, kind="ExternalInput").ap()
        d["wpo"] = nc.dram_tensor(f"wpo{l}", [D, D], F32, kind="ExternalInput").ap()
        d["wo"] = nc.dram_tensor(f"wo{l}", [D, D], F32, kind="ExternalInput").ap()
        d["f2gu"] = nc.dram_tensor(f"f2gu{l}", [D, 2 * DFF], F32, kind="ExternalInput").ap()
        d["f2d"] = nc.dram_tensor(f"f2d{l}", [DFF, D], F32, kind="ExternalInput").ap()
        W.append(d)

    def kview(ap2d, c0, c1):
        return ap2d.rearrange("(k p) c -> p k c", p=128)[:, :, c0:c1]

    P = Prog()
    es = ExitStack()
    with es:
        def sb(name, shape, dt):
            return es.enter_context(nc.sbuf_tensor("sb_" + name, shape, dt))

        xT = sb("xT", [128, KC, tt], F32)
        hT = sb("hT", [128, KC, tt], BF16)
        arena = sb("arena", [128, 24, tt], BF16)
        hout = sb("hout", [128, KC * tt], F32)
        houtv = hout[:].rearrange("p (c t) -> p c t", t=tt)
        stagev = hout[:].rearrange("p (s d) -> p s d", d=D)
        tmps = [sb(f"tmp{i}", [128, 512], F32) for i in range(ntmp)]
        sqs = [sb(f"sq{i}", [128, 512], BF16) for i in range(4)]
        cvb = [sb(f"cv{i}", [128, 2 + 512], F32) for i in range(2)]
        pbs = [[sb(f"pb{i}_{j}", [128, 16 + 512], F32) for j in range(3)] for i in range(2)]
        slots = [sb(f"ws{i}", [128, 4096], BF16) for i in range(nslots)]
        ones = sb("ones", [128, 128], BF16)
        ident = sb("ident", [128, 128], F32)
        sm = sb("sm", [128, SM_PER_LAYER * depth], F32)
        gp05 = sb("gp05", [128, depth * 24], F32)
        bps = sb("bps", [128, depth * 8], F32)
        invc = sb("invc", [128, 4 * 16], F32)
        st_cv = sb("st_cv", [128, depth * KC * 2], F32)
        st_p = sb("st_p", [128, depth * KC * 16], F32)
        psb = [es.enter_context(nc.psum_tensor(f"ps{i}", [128, 512], F32)) for i in range(8)]

        sem_names = list(ENGS) + [f"w{i}" for i in range(nslots)] + ["xin0", "xin1", "cst", "cst2"] + [f"xo{i}" for i in range(ntmp)]
        sems = {n: es.enter_context(nc.semaphore(f"s_{n}")) for n in sem_names}

        rot = {"ps": 0, "tmp": 0, "sq": 0, "cv": 0, "pb": 0}

        def next_ps():
            i = rot["ps"]
            rot["ps"] = (i + 1) % 6
            return i

        def next_tmp():
            i = rot["tmp"]
            rot["tmp"] = (i + 1) % ntmp
            return i

        def next_sq():
            i = rot["sq"]
            rot["sq"] = (i + 1) % 4
            return i

        def tsl(i):
            return slice(i * 512, (i + 1) * 512)

        sched = []

        def flat(c0, c1):
            return lambda s: s[:, c0:c1].rearrange("p (k c) -> p k c", k=KC)

        def sub3(n, j):
            return lambda s: s[:, 0:KC * n * 128].rearrange("p (k n c) -> p k n c", k=KC, n=n)[:, :, j, :]

        def ffn_blocks(l, which):
            gu = W[l]["f1gu" if which == 1 else "f2gu"]
            dn = W[l]["f1d" if which == 1 else "f2d"]
            for jb in range(FC // 2):
                sched.append((("gu", l, which, jb), [
                    (lambda s: s[:, 0:4096].rearrange("p (k n c) -> p k n c", k=KC, n=2)[:, :, 0, :],
                     kview(gu, jb * 256, jb * 256 + 256)),
                    (lambda s: s[:, 0:4096].rearrange("p (k n c) -> p k n c", k=KC, n=2)[:, :, 1, :],
                     kview(gu, DFF + jb * 256, DFF + jb * 256 + 256)),
                ]))
            for i in range(NT):
                for c in range(KC):
                    sched.append((("dn", l, which, c, i), [
                        (lambda s: s[:, 0:FC * 128].rearrange("p (k c) -> p k c", k=FC),
                         kview(dn, c * 128, c * 128 + 128)),
                    ]))

        def mixer_blocks(l):
            win = W[l]["win"]
            for c in range(KC):
                sched.append((("m12", l, c), [
                    (sub3(4, s_), kview(win, s_ * D + c * 128, s_ * D + c * 128 + 128)) for s_ in range(4)
                ]))
            sched.append((("m3", l), [
                (lambda s: s[:, 0:2048].rearrange("p (k c) -> p k c", k=8), kview(W[l]["wpg"], 0, 256))]))
            for c in range(KC):
                sched.append((("m4", l, c), [
                    (sub3(4, 0), kview(W[l]["wco"], c * 128, c * 128 + 128)),
                    (sub3(4, 1), kview(W[l]["wpo"], c * 128, c * 128 + 128)),
                    (sub3(4, 2), kview(win, 4 * D + c * 128, 4 * D + c * 128 + 128)),
                    (sub3(4, 3), kview(win, 5 * D + c * 128, 5 * D + c * 128 + 128)),
                ]))
            for i in range(NT):
                for ob in range(2):
                    sched.append((("m5", l, ob, i), [
                        (flat(0, 4096), kview(W[l]["wo"], ob * 512, ob * 512 + 512))]))

        n_tt = ntok // tt
        for _t in range(n_tt):
            for l in range(depth):
                ffn_blocks(l, 1)
                mixer_blocks(l)
                ffn_blocks(l, 2)

        wstate = {"cur": 0, "emitted": 0, "released": 0}

        def emit_wdma(m):
            tag, dmas = sched[m]
            s = m % nslots
            for dst_fn, src in dmas:
                dst = dst_fn(slots[s])
                P.emit("pool", (lambda e, dst=dst, src=src: e.dma_start(out=dst, in_=src)),
                       writes=[("w", s)], stream=f"w{s}")

        def prefetch():
            upto = min(wstate["released"] + nslots, len(sched))
            for m in range(wstate["emitted"], upto):
                emit_wdma(m)
            wstate["emitted"] = max(wstate["emitted"], upto)

        def acquire(tag):
            n = wstate["cur"]
            assert sched[n][0] == tag, (sched[n][0], tag)
            assert n < wstate["emitted"], "weight block not prefetched"
            wstate["cur"] = n + 1
            return n % nslots

        def release(count=1):
            wstate["released"] += count
            assert wstate["released"] <= wstate["cur"]
            prefetch()

        def mm_group(bank, pairs, reads):
            n = len(pairs)

            def fn(e, pairs=pairs, bank=bank, n=n):
                inst = None
                for q, (l_, r_) in enumerate(pairs):
                    inst = e.matmul(out=psb[bank][:], lhsT=l_, rhs=r_, start=(q == 0), stop=(q == n - 1))
                return inst
            P.emit("pe", fn, reads=reads, writes=[("ps", bank)])

        def act(out, in_, func, reads, writes, bias=None, scale=None):
            kw = {}
            if bias is not None:
                kw["bias"] = bias
            if scale is not None:
                kw["scale"] = scale
            P.emit("act", (lambda e: e.activation(out=out, in_=in_, func=func, **kw)), reads=reads, writes=writes)

        def dve_tt(out, in0, in1, op, reads, writes):
            P.emit("dve", (lambda e: e.tensor_tensor(out=out, in0=in0, in1=in1, op=op)), reads=reads, writes=writes)

        def dve_stt(out, in0, scalar, in1, op0, op1, reads, writes):
            P.emit("dve", (lambda e: e.scalar_tensor_tensor(out=out, in0=in0, scalar=scalar, in1=in1,
                                                            op0=op0, op1=op1)), reads=reads, writes=writes)

        def dve_copy(out, in_, reads, writes):
            P.emit("dve", (lambda e: e.tensor_copy(out=out, in_=in_)), reads=reads, writes=writes)

        def smc(l, col):
            c = l * SM_PER_LAYER + col
            return sm[:, c:c + 1]

        NPS = [6, 7]

        def emit_rstd(i):
            b = NPS[i]
            if RSTD_MODE == "lnexp":
                t = next_tmp()
                act(tmps[t][:], psb[b][:], AF.Ln, reads=[("ps", b), ("eps",)], writes=[("tmp", t)],
                    bias=EPS_AP[0], scale=1.0 / D)
                act(psb[b][:], tmps[t][:], AF.Exp, reads=[("tmp", t)], writes=[("ps", b)], scale=-0.5)
            else:
                t = next_tmp()
                act(tmps[t][:], psb[b][:], AF.Sqrt, reads=[("ps", b), ("eps",)], writes=[("tmp", t)],
                    bias=EPS_AP[0], scale=1.0 / D)
                P.emit("dve", (lambda e: e.reciprocal(out=psb[b][:], in_=tmps[t][:])),
                       reads=[("tmp", t)], writes=[("ps", b)])

        def ss_mm(i, q, first, last):
            b = NPS[i]

            def fn(e):
                return e.matmul(out=psb[b][:], lhsT=ones[:], rhs=sqs[q][:], start=first, stop=last)
            P.emit("pe", fn, reads=[("sq", q), ("ones",)], writes=[("ps", b)])

        def pre_norm_tile(l, gcol, i):
            pend = None
            for k in range(KC):
                q = next_sq()
                act(sqs[q][:], xT[:, k, tsl(i)], AF.Square, reads=[("xT", k, i)], writes=[("sq", q)])
                if pend is not None:
                    ss_mm(i, *pend)
                pend = (q, k == 0, k == KC - 1)
            ss_mm(i, *pend)
            emit_rstd(i)
            b = NPS[i]
            for k in range(KC):
                dve_stt(hT[:, k, tsl(i)], xT[:, k, tsl(i)], smc(l, gcol + k), psb[b][:], ALU.mult, ALU.mult,
                        reads=[("xT", k, i), ("ps", b), ("sm",)], writes=[("hT", k, i)])

        def post_evac(bank, c, i, pend):
            act(houtv[:, c, tsl(i)], psb[bank][:], AF.Copy, reads=[("ps", bank)], writes=[("hout", c, i)])
            q = next_sq()
            act(sqs[q][:], psb[bank][:], AF.Square, reads=[("ps", bank)], writes=[("sq", q)])
            if pend[i] is not None:
                ss_mm(i, *pend[i])
            pend[i] = (q, c == 0, c == KC - 1)

        def tail(l, gidx, pend, i):
            ss_mm(i, *pend[i])
            pend[i] = None
            emit_rstd(i)
            b = NPS[i]
            for c in range(KC):
                t = next_tmp()
                gc = l * 24 + gidx * 8 + c
                dve_stt(tmps[t][:], houtv[:, c, tsl(i)], gp05[:, gc:gc + 1], psb[b][:], ALU.mult, ALU.mult,
                        reads=[("hout", c, i), ("ps", b), ("gp05",)], writes=[("tmp", t)])
                dve_tt(xT[:, c, tsl(i)], xT[:, c, tsl(i)], tmps[t][:], ALU.add,
                       reads=[("xT", c, i), ("tmp", t)], writes=[("xT", c, i)])

        GRP = 3

        def hreads(i):
            return [("hT", k, i) for k in range(KC)]

        def ffn(l, which, hook_i1, next_pre):
            def up_block(s, jb, i):
                sv = slots[s][:, 0:4096].rearrange("p (k n c) -> p k n c", k=KC, n=2)
                for jj in range(2):
                    j = 2 * jb + jj
                    bg, bu = next_ps(), next_ps()
                    mm_group(bg, [(sv[:, k, 0, jj * 128:(jj + 1) * 128], hT[:, k, tsl(i)]) for k in range(KC)],
                             reads=[("w", s)] + hreads(i))
                    mm_group(bu, [(sv[:, k, 1, jj * 128:(jj + 1) * 128], hT[:, k, tsl(i)]) for k in range(KC)],
                             reads=[("w", s)] + hreads(i))
                    t = next_tmp()
                    act(tmps[t][:], psb[bg][:], AF.Silu, reads=[("ps", bg)], writes=[("tmp", t)])
                    dve_tt(arena[:, j, tsl(i)], tmps[t][:], psb[bu][:], ALU.mult,
                           reads=[("tmp", t), ("ps", bu)], writes=[("ar", j, i)])

            gs = [acquire(("gu", l, which, jb)) for jb in range(GRP)]
            for i in range(NT):
                for jb in range(GRP):
                    up_block(gs[jb], jb, i)
                    if i == 0 and jb == 1 and hook_i1 is not None:
                        hook_i1()
            release(GRP)
            for jb in range(GRP, FC // 2):
                s = acquire(("gu", l, which, jb))
                for i in range(NT):
                    up_block(s, jb, i)
                release()
            pend = [None] * NT
            for i in range(NT):
                for c in range(KC):
                    s = acquire(("dn", l, which, c, i))
                    sv = slots[s][:, 0:FC * 128].rearrange("p (k c) -> p k c", k=FC)
                    bo = next_ps()
                    mm_group(bo, [(sv[:, j, :], arena[:, j, tsl(i)]) for j in range(FC)],
                             reads=[("w", s)] + [("ar", j, i) for j in range(FC)])
                    post_evac(bo, c, i, pend)
                    release()
                    if i == NT - 1 and c == 3 and next_pre is not None:
                        ss_mm(i, *pend[i])
                        pend[i] = None
                        next_pre(0)
                tail(l, 0 if which == 1 else 2, pend, i)

        def mixer(l, seq_start, hook_i1, next_pre):
            UA, DM, PG = 0, 8, 16

            def act_copy(out, in_, reads, writes):
                act(out, in_, AF.Copy, reads=reads, writes=writes)

            def m12_block(s, c, i):
                sv = slots[s][:, 0:4096].rearrange("p (k n c) -> p k n c", k=KC, n=4)
                stc = st_cv[:, (l * KC + c) * 2:(l * KC + c) * 2 + 2]
                g = c // 2
                win = POOL_WINDOWS[g]
                stp = st_p[:, (l * KC + c) * 16:(l * KC + c) * 16 + 16]
                bv, bc, bp, bb = next_ps(), next_ps(), next_ps(), next_ps()
                for n_, bk in ((2, bv), (1, bc), (3, bp), (0, bb)):
                    mm_group(bk, [(sv[:, k, n_, :], hT[:, k, tsl(i)]) for k in range(KC)],
                             reads=[("w", s)] + hreads(i))
                tv = next_tmp()
                act(tmps[tv][:], psb[bv][:], AF.Copy, reads=[("ps", bv)], writes=[("tmp", tv)])
                tb = next_tmp()
                act(tmps[tb][:], psb[bb][:], AF.Copy, reads=[("ps", bb)], writes=[("tmp", tb)])
                cv = rot["cv"]
                rot["cv"] = 1 - cv
                cb = cvb[cv]
                act_copy(cb[:, 0:2], stc, reads=[("stcv", l, c)], writes=[("cv", cv)])
                dve_tt(cb[:, 2:514], tmps[tv][:], psb[bc][:], ALU.mult,
                       reads=[("tmp", tv), ("ps", bc)], writes=[("cv", cv)])
                act_copy(stc, cb[:, 512:514], reads=[("cv", cv)], writes=[("stcv", l, c)])
                t0, t1, t2 = next_tmp(), next_tmp(), next_tmp()
                act(tmps[t0][:], cb[:, 0:512], AF.Copy, reads=[("cv", cv), ("sm",)], writes=[("tmp", t0)],
                    scale=smc(l, SM_CW + 0 * 8 + c))
                dve_stt(tmps[t1][:], cb[:, 1:513], smc(l, SM_CW + 1 * 8 + c), tmps[t0][:], ALU.mult, ALU.add,
                        reads=[("cv", cv), ("tmp", t0), ("sm",)], writes=[("tmp", t1)])
                dve_stt(tmps[t2][:], cb[:, 2:514], smc(l, SM_CW + 2 * 8 + c), tmps[t1][:], ALU.mult, ALU.add,
                        reads=[("cv", cv), ("tmp", t1), ("sm",)], writes=[("tmp", t2)])
                dve_tt(arena[:, UA + c, tsl(i)], tmps[t2][:], tmps[tb][:], ALU.mult,
                       reads=[("tmp", t2), ("tmp", tb)], writes=[("ar", UA + c, i)])
                pi = rot["pb"]
                rot["pb"] = 1 - pi
                pbuf, sA, sB = pbs[pi]
                act_copy(pbuf[:, 0:16], stp, reads=[("stp", l, c)], writes=[("pb", pi, 0)])
                act(pbuf[:, 16:528], psb[bp][:], AF.Copy, reads=[("ps", bp)], writes=[("pb", pi, 0)])
                act_copy(stp, pbuf[:, 512:528], reads=[("pb", pi, 0)], writes=[("stp", l, c)])
                src, dst, skey, dkey = pbuf, sA, ("pb", pi, 0), ("pb", pi, 1)
                step, lo = 1, 1
                while step < win:
                    dve_tt(dst[:, lo:528], src[:, lo:528], src[:, lo - step:528 - step], ALU.add,
                           reads=[skey], writes=[dkey])
                    step *= 2
                    lo = 2 * step - 1
                    if src is pbuf:
                        src, skey = sA, ("pb", pi, 1)
                        dst, dkey = sB, ("pb", pi, 2)
                    else:
                        src, dst = dst, src
                        skey, dkey = dkey, skey
                fin, fkey = src, skey
                dve_stt(arena[:, DM + c, tsl(i)], fin[:, 16:528], 1.0 / win, pbuf[:, 16:528],
                        ALU.mult, ALU.subtract, reads=[fkey, ("pb", pi, 0)], writes=[("ar", DM + c, i)])
                if seq_start and i == 0:
                    t = next_tmp()
                    dve_tt(tmps[t][:, 0:16], fin[:, 16:32], invc[:, g * 16:(g + 1) * 16], ALU.mult,
                           reads=[fkey, ("invc",)], writes=[("tmp", t)])
                    dve_tt(arena[:, DM + c, 0:16], tmps[t][:, 0:16], pbuf[:, 16:32], ALU.subtract,
                           reads=[("tmp", t), ("pb", pi, 0)], writes=[("ar", DM + c, i)])

            gs = [acquire(("m12", l, c)) for c in range(GRP)]
            for i in range(NT):
                for c in range(GRP):
                    m12_block(gs[c], c, i)
                    if i == 0 and c == 1 and hook_i1 is not None:
                        hook_i1()
            release(GRP)
            for c in range(GRP, KC):
                s = acquire(("m12", l, c))
                for i in range(NT):
                    m12_block(s, c, i)
                release()
            s = acquire(("m3", l))
            sv = slots[s][:, 0:2048].rearrange("p (k c) -> p k c", k=8)
            for g in range(4):
                for jj in range(2):
                    c = 2 * g + jj
                    for i in range(NT):
                        bk = next_ps()
                        mm_group(bk, [(sv[:, 2 * g + kk, jj * 128:(jj + 1) * 128], arena[:, DM + 2 * g + kk, tsl(i)])
                                      for kk in range(2)],
                                 reads=[("w", s)] + [("ar", DM + 2 * g + kk, i) for kk in range(2)])
                        act(arena[:, PG + c, tsl(i)], psb[bk][:], AF.Identity,
                            reads=[("ps", bk), ("bps",), ("sm",)], writes=[("ar", PG + c, i)],
                            bias=bps[:, l * 8 + c:l * 8 + c + 1], scale=smc(l, SM_PS + c))
            release()
            for c in range(KC):
                s = acquire(("m4", l, c))
                sv = slots[s][:, 0:4096].rearrange("p (k n c) -> p k n c", k=KC, n=4)
                for i in range(NT):
                    res = []
                    for br, (wsel, gsel, src_off, bcol) in enumerate(((0, 2, UA, SM_BG + c), (1, 3, PG, SM_BG + 8 + c))):
                        by, bgt = next_ps(), next_ps()
                        mm_group(by, [(sv[:, k, wsel, :], arena[:, src_off + k, tsl(i)]) for k in range(KC)],
                                 reads=[("w", s)] + [("ar", src_off + k, i) for k in range(KC)])
                        mm_group(bgt, [(sv[:, k, gsel, :], hT[:, k, tsl(i)]) for k in range(KC)],
                                 reads=[("w", s)] + hreads(i))
                        tg, ty = next_tmp(), next_tmp()
                        act(tmps[tg][:], psb[bgt][:], AF.Sigmoid, reads=[("ps", bgt), ("sm",)], writes=[("tmp", tg)],
                            bias=smc(l, bcol))
                        dve_tt(tmps[ty][:], tmps[tg][:], psb[by][:], ALU.mult,
                               reads=[("tmp", tg), ("ps", by)], writes=[("tmp", ty)])
                        res.append(ty)
                    dve_tt(arena[:, DM + c, tsl(i)], tmps[res[0]][:], tmps[res[1]][:], ALU.add,
                           reads=[("tmp", res[0]), ("tmp", res[1])], writes=[("ar", DM + c, i)])
                release()
            pend = [None] * NT
            for i in range(NT):
                for ob in range(2):
                    s = acquire(("m5", l, ob, i))
                    sv = slots[s][:, 0:4096].rearrange("p (k c) -> p k c", k=KC)
                    for cc in range(4):
                        c = 4 * ob + cc
                        bo = next_ps()
                        mm_group(bo, [(sv[:, k, cc * 128:(cc + 1) * 128], arena[:, DM + k, tsl(i)]) for k in range(KC)],
                                 reads=[("w", s)] + [("ar", DM + k, i) for k in range(KC)])
                        post_evac(bo, c, i, pend)
                    release()
                    if i == NT - 1 and ob == 0 and next_pre is not None:
                        ss_mm(i, *pend[i])
                        pend[i] = None
                        next_pre(0)
                tail(l, 1, pend, i)

        P.emit("sp", (lambda e: e.dma_start(out=sm[:], in_=sm_d[:, :])), writes=[("sm",)], stream="cst")
        P.emit("sp", (lambda e: e.dma_start(out=ident[:], in_=id_d[:, :])), writes=[("ident",)], stream="cst2")
        P.emit("dve", (lambda e: e.memset(ones[:], 1.0)), writes=[("ones",)])
        for g, win in enumerate(POOL_WINDOWS):
            P.emit("dve", (lambda e, g=g, win=win: e.memset(invc[:, g * 16:(g + 1) * 16], 1.0 / win)), writes=[("invc",)])
            for t_ in range(win - 1):
                P.emit("dve", (lambda e, g=g, t_=t_: e.memset(invc[:, g * 16 + t_:g * 16 + t_ + 1], 1.0 / (t_ + 1))),
                       writes=[("invc",)])
        eps_t = sb("eps_t", [128, 1], F32)
        EPS_AP = [eps_t[:, 0:1]]
        P.emit("dve", (lambda e: e.memset(eps_t[:], EPS)), writes=[("eps",)])
        for l in range(depth):
            for gidx, (col, fac) in enumerate(((SM_F1POST, 0.5), (SM_MPOST, 1.0), (SM_F2POST, 0.5))):
                o = l * 24 + gidx * 8
                c0 = l * SM_PER_LAYER + col
                P.emit("dve", (lambda e, o=o, c0=c0, fac=fac: e.tensor_scalar(
                    out=gp05[:, o:o + 8], in0=sm[:, c0:c0 + 8], scalar1=fac, scalar2=None, op0=ALU.mult)),
                    reads=[("sm",)], writes=[("gp05",)])
            c0 = l * SM_PER_LAYER
            P.emit("dve", (lambda e, l=l, c0=c0: e.tensor_tensor(
                out=bps[:, l * 8:l * 8 + 8], in0=sm[:, c0 + SM_BPG:c0 + SM_BPG + 8],
                in1=sm[:, c0 + SM_PS:c0 + SM_PS + 8], op=ALU.mult)), reads=[("sm",)], writes=[("bps",)])

        prefetch()
        tiles_per_seq = seq // tt
        assert NT == 2

        def xload(ti, i):
            tok = ti * tt + i * 512
            for h in range(2):
                src = x_d[tok:tok + 512, h * 512:(h + 1) * 512].rearrange("(s p) t -> p s t", p=128)
                dst = houtv[:, :, tsl(i)].rearrange("p (s h) t -> p s h t", h=2)[:, :, h, :]
                P.emit("sp", (lambda e, dst=dst, src=src: e.dma_start(out=dst, in_=src)),
                       writes=[("hout", c, i) for c in range(KC)], stream=f"xin{i}")

        def in_transposes(i):
            for k in range(KC):
                h, t0 = k // 4, (k % 4) * 128
                bk = next_ps()

                def fn(e, k=k, h=h, t0=t0, bk=bk):
                    inst = None
                    for s4 in range(4):
                        inst = e.transpose(out=psb[bk][:, s4 * 128:(s4 + 1) * 128],
                                           in_=houtv[:, 2 * s4 + h, i * 512 + t0:i * 512 + t0 + 128],
                                           identity=ident[:])
                    return inst
                P.emit("pe", fn, reads=[("hout", c, i) for c in range(KC)] + [("ident",)], writes=[("ps", bk)])
                if k % 2 == 0:
                    act(xT[:, k, tsl(i)], psb[bk][:], AF.Copy, reads=[("ps", bk)], writes=[("xT", k, i)])
                else:
                    dve_copy(xT[:, k, tsl(i)], psb[bk][:], reads=[("ps", bk)], writes=[("xT", k, i)])

        def output_half(ti, i):
            for s4 in range(4):
                for h in range(2):
                    bk = next_ps()

                    def fn(e, s4=s4, h=h, bk=bk):
                        inst = None
                        for k4 in range(4):
                            k = h * 4 + k4
                            inst = e.transpose(out=psb[bk][:, k4 * 128:(k4 + 1) * 128],
                                               in_=xT[:, k, i * 512 + s4 * 128:i * 512 + (s4 + 1) * 128],
                                               identity=ident[:])
                        return inst
                    P.emit("pe", fn, reads=[("xT", h * 4 + k4, i) for k4 in range(4)] + [("ident",)],
                           writes=[("ps", bk)])
                    t = next_tmp()
                    if (s4 + h) % 2 == 0:
                        act(tmps[t][:], psb[bk][:], AF.Copy, reads=[("ps", bk)], writes=[("tmp", t)])
                    else:
                        dve_copy(tmps[t][:], psb[bk][:], reads=[("ps", bk)], writes=[("tmp", t)])
                    r0 = ti * tt + i * 512 + s4 * 128
                    P.emit("sp", (lambda e, t=t, r0=r0, h=h: e.dma_start(
                        out=out_d[r0:r0 + 128, h * 512:(h + 1) * 512], in_=tmps[t][:])),
                        reads=[("tmp", t)], stream=f"xo{t}")

        subs = []
        for l in range(depth):
            subs += [("ffn", l, 1), ("mix", l, 0), ("ffn", l, 2)]

        def pre_fn(sub):
            kind, l_, which = sub
            gcol = SM_MPRE if kind == "mix" else (SM_F1PRE if which == 1 else SM_F2PRE)
            return lambda i, l_=l_, gcol=gcol: pre_norm_tile(l_, gcol, i)

        for ti in range(n_tt):
            seq_start = (ti % tiles_per_seq == 0)
            if ti == 0:
                xload(0, 0)
                xload(0, 1)
                in_transposes(0)
                pre_fn(subs[0])(0)
            if seq_start:
                P.emit("dve", (lambda e: e.memset(st_cv[:], 0.0)),
                       writes=[("stcv", l, c) for l in range(depth) for c in range(KC)])
                P.emit("dve", (lambda e: e.memset(st_p[:], 0.0)),
                       writes=[("stp", l, c) for l in range(depth) for c in range(KC)])
            for si, sub in enumerate(subs):
                mypre = pre_fn(sub)
                first, last = (si == 0), (si == len(subs) - 1)

                def hook(mypre=mypre, first=first, ti=ti):
                    if first:
                        if ti > 0:
                            xload(ti, 1)
                            output_half(ti - 1, 1)
                        in_transposes(1)
                    mypre(1)

                if not last:
                    nxt = pre_fn(subs[si + 1])
                else:
                    def nxt(i, ti=ti):
                        assert i == 0
                        if ti + 1 < n_tt:
                            xload(ti + 1, 0)
                        output_half(ti, 0)
                        if ti + 1 < n_tt:
                            in_transposes(0)
                            pre_fn(subs[0])(0)
                if sub[0] == "ffn":
                    ffn(sub[1], sub[2], hook, nxt)
                else:
                    mixer(sub[1], seq_start, hook, nxt)
        output_half(n_tt - 1, 1)
        P.emit("sp", None, reads=[], writes=[("tmp", t) for t in range(ntmp)])
        assert wstate["cur"] == len(sched)

        with nc.Block() as block:
            @block.tensor
            def _(e):
                P.replay("pe", e, sems)

            @block.scalar
            def _(e):
                P.replay("act", e, sems)

            @block.vector
            def _(e):
                P.replay("dve", e, sems)

            @block.gpsimd
            def _(e):
                P.replay("pool", e, sems)

            @block.sync
            def _(e):
                P.replay("sp", e, sems)
    return nc


def pack_smalls(inp, depth):
    sm = np.zeros((128, SM_PER_LAYER * depth), np.float32)

    def put(l, col, vec):
        v = np.asarray(vec, np.float32).reshape(-1, 128)
        sm[:, l * SM_PER_LAYER + col:l * SM_PER_LAYER + col + v.shape[0]] = v.T

    for l in range(depth):
        put(l, SM_F1PRE, inp["ffn1_pre"][l])
        put(l, SM_F1POST, inp["ffn1_post"][l])
        put(l, SM_MPRE, inp["mix_pre"][l])
        put(l, SM_MPOST, inp["mix_post"][l])
        put(l, SM_F2PRE, inp["ffn2_pre"][l])
        put(l, SM_F2POST, inp["ffn2_post"][l])
        put(l, SM_BG, inp["b_gate"][l])
        put(l, SM_CW, inp["conv_w"][l])
        put(l, SM_BPG, inp["b_pool_group"][l])
        put(l, SM_PS, inp["pool_scale"][l])
    return sm


def make_in_maps(inp, depth, ncores, xs):
    common = {"smalls": pack_smalls(inp, depth), "ident": np.eye(128, dtype=np.float32)}
    for l in range(depth):
        common[f"f1gu{l}"] = np.ascontiguousarray(inp["ffn1_w_gate_up"][l], np.float32)
        common[f"f1d{l}"] = np.ascontiguousarray(inp["ffn1_w_down"][l], np.float32)
        common[f"win{l}"] = np.ascontiguousarray(inp["w_in"][l], np.float32)
        common[f"wco{l}"] = np.ascontiguousarray(inp["w_conv_out"][l], np.float32)
        common[f"wpg{l}"] = np.ascontiguousarray(inp["w_pool_group"][l], np.float32).reshape(4 * 256, 256)
        common[f"wpo{l}"] = np.ascontiguousarray(inp["w_pool_out"][l], np.float32)
        common[f"wo{l}"] = np.ascontiguousarray(inp["w_o"][l], np.float32)
        common[f"f2gu{l}"] = np.ascontiguousarray(inp["ffn2_w_gate_up"][l], np.float32)
        common[f"f2d{l}"] = np.ascontiguousarray(inp["ffn2_w_down"][l], np.float32)
    maps = []
    for c in range(ncores):
        m = dict(common)
        m["x"] = xs[c]
        maps.append(m)
    return maps


_NC_CACHE = {}


def kernel(**inputs):
    inp = {k: np.asarray(v) for k, v in inputs.items()}
    x = inp["x"].astype(np.float32, copy=False)
    B, S, _ = x.shape
    depth = inp["ffn1_pre"].shape[0]
    nseq = B // NCORES
    xs = [np.ascontiguousarray(x[c * nseq:(c + 1) * nseq].reshape(nseq * S, D)) for c in range(NCORES)]
    key = (nseq, S, depth)
    if key not in _NC_CACHE:
        _NC_CACHE[key] = build_program(nseq, S, depth)
    nc = _NC_CACHE[key]
    res = run_bass_kernel_spmd(nc, make_in_maps(inp, depth, NCORES, xs), core_ids=list(range(NCORES)))
    out = np.stack([r["out"].reshape(nseq, S, D) for r in res.results], axis=0).reshape(B, S, D)
    return out.astype(np.float32, copy=False)
```

```python
import bisect
from contextlib import ExitStack

import numpy as np
import concourse.bass as bass
import concourse.mybir as mybir
from concourse.bass_utils import run_bass_kernel_spmd

F32 = mybir.dt.float32
BF16 = mybir.dt.bfloat16
AF = mybir.ActivationFunctionType
ALU = mybir.AluOpType

D = 1024
KC = 8
DFF = 2816
FC = 22
EPS = 1e-6
NCORES = 8
POOL_WINDOWS = (2, 4, 8, 16)
RSTD_MODE = "lnexp"

ENGS = ("pe", "act", "dve", "pool", "sp")

SM_F1PRE, SM_F1POST, SM_MPRE, SM_MPOST, SM_F2PRE, SM_F2POST = 0, 8, 16, 24, 32, 40
SM_BG = 48
SM_CW = 64
SM_BPG = 88
SM_PS = 96
SM_PER_LAYER = 104


class Prog:
    def __init__(self):
        self.ins = {e: [] for e in ENGS}
        self.res = {}
        self.seen = {e: {} for e in ENGS}
        self.scount = {}
        self.milestones = {e: set() for e in ENGS}

    def emit(self, eng, fn, reads=(), writes=(), stream=None):
        idx = len(self.ins[eng])
        if stream is None:
            me = (eng, idx)
        else:
            sidx = self.scount.get(stream, 0)
            self.scount[stream] = sidx + 1
            me = (stream, sidx)
        deps = {}

        def add(dep, kind):
            if dep is None:
                return
            src, i = dep
            if stream is None and src == eng:
                if eng == "pe" or kind == "war":
                    return
            if i > deps.get(src, -1):
                deps[src] = i

        for r in reads:
            st = self.res.get(r)
            if st is not None:
                add(st["w"], "raw")
        for w in writes:
            st = self.res.get(w)
            if st is not None:
                add(st["w"], "waw")
                for rd in st["r"]:
                    add(rd, "war")
        waits = []
        for src, i in deps.items():
            if i > self.seen[eng].get(src, -1):
                self.seen[eng][src] = i
                waits.append((src, i))
                if src in self.milestones:
                    self.milestones[src].add(i)
        for r in reads:
            st = self.res.setdefault(r, {"w": None, "r": []})
            st["r"].append(me)
        for w in writes:
            self.res[w] = {"w": me, "r": []}
        self.ins[eng].append((waits, fn, stream, idx))
        return me

    def replay(self, eng, e, sems):
        ms = sorted(self.milestones[eng])
        msorted = {s: sorted(self.milestones[s]) for s in self.milestones}

        def semval(src, i):
            if src in msorted:
                return bisect.bisect_right(msorted[src], i)
            return 16 * (i + 1)

        msset = set(ms)
        for waits, fn, stream, idx in self.ins[eng]:
            for src, i in waits:
                e.wait_ge(sems[src], semval(src, i))
            if fn is None:
                continue
            inst = fn(e)
            if stream is not None:
                inst.then_inc(sems[stream], 16)
            elif idx in msset:
                inst.then_inc(sems[eng], 1)


def build_program(nseq, seq, depth, tt=1024, nslots=4, ntmp=11):
    NT = tt // 512
    assert tt % 512 == 0 and seq % tt == 0
    ntok = nseq * seq
    nsub = tt // 128
    nc = bass.Bass("TRN2", target_bir_lowering=False)

    x_d = nc.dram_tensor("x", [ntok, D], F32, kind="ExternalInput").ap()
    out_d = nc.dram_tensor("out", [ntok, D], F32, kind="ExternalOutput").ap()
    sm_d = nc.dram_tensor("smalls", [128, SM_PER_LAYER * depth], F32, kind="ExternalInput").ap()
    id_d = nc.dram_tensor("ident", [128, 128], F32, kind="ExternalInput").ap()
    W = []
    for l in range(depth):
        d = {}
        d["f1gu"] = nc.dram_tensor(f"f1gu{l}", [D, 2 * DFF], F32, kind="ExternalInput").ap()
        d["f1d"] = nc.dram_tensor(f"f1d{l}", [DFF, D], F32, kind="ExternalInput").ap()
        d["m12w"] = nc.dram_tensor(f"m12w{l}", [D, 4 * D], F32, kind="ExternalInput").ap()
        d["m4w"] = nc.dram_tensor(f"m4w{l}", [D, 4 * D], F32, kind="ExternalInput").ap()
        d["wpg"] = nc.dram_tensor(f"wpg{l}", [4 * 256, 256], F32, kind="ExternalInput").ap()
        d["wo"] = nc.dram_tensor(f"wo{l}", [D, D], F32, kind="ExternalInput").ap()
        d["f2gu"] = nc.dram_tensor(f"f2gu{l}", [D, 2 * DFF], F32, kind="ExternalInput").ap()
        d["f2d"] = nc.dram_tensor(f"f2d{l}", [DFF, D], F32, kind="ExternalInput").ap()
        W.append(d)

    def kview(ap2d, c0, c1):
        return ap2d.rearrange("(k p) c -> p k c", p=128)[:, :, c0:c1]

    P = Prog()
    es = ExitStack()
    with es:
        def sb(name, shape, dt):
            return es.enter_context(nc.sbuf_tensor("sb_" + name, shape, dt))

        xT = sb("xT", [128, KC, tt], F32)
        hT = sb("hT", [128, KC, tt], BF16)
        arena = sb("arena", [128, 24, tt], BF16)
        hout = sb("hout", [128, KC * tt], F32)
        houtv = hout[:].rearrange("p (c t) -> p c t", t=tt)
        stagev = hout[:].rearrange("p (s d) -> p s d", d=D)
        tmps = [sb(f"tmp{i}", [128, 512], F32) for i in range(ntmp)]
        sqs = [sb(f"sq{i}", [128, 512], BF16) for i in range(4)]
        cvb = [sb(f"cv{i}", [128, 2 + 512], F32) for i in range(2)]
        pbs = [[sb(f"pb{i}_{j}", [128, 16 + 512], F32) for j in range(3)] for i in range(2)]
        slots = [sb(f"ws{i}", [128, 4096], BF16) for i in range(nslots)]
        ones = sb("ones", [128, 128], BF16)
        ident = sb("ident", [128, 128], F32)
        sm = sb("sm", [128, SM_PER_LAYER * depth], F32)
        gp05 = sb("gp05", [128, depth * 24], F32)
        bps = sb("bps", [128, depth * 8], F32)
        invc = sb("invc", [128, 4 * 16], F32)
        st_cv = sb("st_cv", [128, depth * KC * 2], F32)
        st_p = sb("st_p", [128, depth * KC * 16], F32)
        psb = [es.enter_context(nc.psum_tensor(f"ps{i}", [128, 512], F32)) for i in range(8)]

        sem_names = list(ENGS) + [f"w{i}" for i in range(nslots)] + ["xin0", "xin1", "cst", "cst2"] + [f"xo{i}" for i in range(ntmp)]
        sems = {n: es.enter_context(nc.semaphore(f"s_{n}")) for n in sem_names}

        rot = {"ps": 0, "tmp": 0, "sq": 0, "cv": 0, "pb": 0}

        def next_ps():
            i = rot["ps"]
            rot["ps"] = (i + 1) % 6
            return i

        def next_tmp():
            i = rot["tmp"]
            rot["tmp"] = (i + 1) % ntmp
            return i

        def next_sq():
            i = rot["sq"]
            rot["sq"] = (i + 1) % 4
            return i

        def tsl(i):
            return slice(i * 512, (i + 1) * 512)

        sched = []

        def flat(c0, c1):
            return lambda s: s[:, c0:c1].rearrange("p (k c) -> p k c", k=KC)

        def sub3(n, j):
            return lambda s: s[:, 0:KC * n * 128].rearrange("p (k n c) -> p k n c", k=KC, n=n)[:, :, j, :]

        def ffn_blocks(l, which):
            gu = W[l]["f1gu" if which == 1 else "f2gu"]
            dn = W[l]["f1d" if which == 1 else "f2d"]
            for jb in range(FC // 2):
                sched.append((("gu", l, which, jb), [
                    (flat(0, 4096), kview(gu, jb * 512, jb * 512 + 512))]))
            for i in range(NT):
                for c in range(KC):
                    sched.append((("dn", l, which, c, i), [
                        (lambda s: s[:, 0:FC * 128].rearrange("p (k c) -> p k c", k=FC),
                         kview(dn, c * 128, c * 128 + 128)),
                    ]))

        def mixer_blocks(l):
            for c in range(KC):
                sched.append((("m12", l, c), [
                    (flat(0, 4096), kview(W[l]["m12w"], c * 512, c * 512 + 512))]))
            sched.append((("m3", l), [
                (lambda s: s[:, 0:2048].rearrange("p (k c) -> p k c", k=8), kview(W[l]["wpg"], 0, 256))]))
            for c in range(KC):
                sched.append((("m4", l, c), [
                    (flat(0, 4096), kview(W[l]["m4w"], c * 512, c * 512 + 512))]))
            for i in range(NT):
                for ob in range(2):
                    sched.append((("m5", l, ob, i), [
                        (flat(0, 4096), kview(W[l]["wo"], ob * 512, ob * 512 + 512))]))

        n_tt = ntok // tt
        for _t in range(n_tt):
            for l in range(depth):
                ffn_blocks(l, 1)
                mixer_blocks(l)
                ffn_blocks(l, 2)

        wstate = {"cur": 0, "emitted": 0, "released": 0}

        def emit_wdma(m):
            tag, dmas = sched[m]
            s = m % nslots
            for dst_fn, src in dmas:
                dst = dst_fn(slots[s])
                P.emit("pool", (lambda e, dst=dst, src=src: e.dma_start(out=dst, in_=src)),
                       writes=[("w", s)], stream=f"w{s}")

        def prefetch():
            upto = min(wstate["released"] + nslots, len(sched))
            for m in range(wstate["emitted"], upto):
                emit_wdma(m)
            wstate["emitted"] = max(wstate["emitted"], upto)

        def acquire(tag):
            n = wstate["cur"]
            assert sched[n][0] == tag, (sched[n][0], tag)
            assert n < wstate["emitted"], "weight block not prefetched"
            wstate["cur"] = n + 1
            return n % nslots

        def release(count=1):
            wstate["released"] += count
            assert wstate["released"] <= wstate["cur"]
            prefetch()

        def mm_group(bank, pairs, reads):
            n = len(pairs)

            def fn(e, pairs=pairs, bank=bank, n=n):
                inst = None
                for q, (l_, r_) in enumerate(pairs):
                    inst = e.matmul(out=psb[bank][:], lhsT=l_, rhs=r_, start=(q == 0), stop=(q == n - 1))
                return inst
            P.emit("pe", fn, reads=reads, writes=[("ps", bank)])

        def act(out, in_, func, reads, writes, bias=None, scale=None):
            kw = {}
            if bias is not None:
                kw["bias"] = bias
            if scale is not None:
                kw["scale"] = scale
            P.emit("act", (lambda e: e.activation(out=out, in_=in_, func=func, **kw)), reads=reads, writes=writes)

        def dve_tt(out, in0, in1, op, reads, writes):
            P.emit("dve", (lambda e: e.tensor_tensor(out=out, in0=in0, in1=in1, op=op)), reads=reads, writes=writes)

        def dve_stt(out, in0, scalar, in1, op0, op1, reads, writes):
            P.emit("dve", (lambda e: e.scalar_tensor_tensor(out=out, in0=in0, scalar=scalar, in1=in1,
                                                            op0=op0, op1=op1)), reads=reads, writes=writes)

        def dve_copy(out, in_, reads, writes):
            P.emit("dve", (lambda e: e.tensor_copy(out=out, in_=in_)), reads=reads, writes=writes)

        def smc(l, col):
            c = l * SM_PER_LAYER + col
            return sm[:, c:c + 1]

        NPS = [6, 7]

        def emit_rstd(i):
            b = NPS[i]
            if RSTD_MODE == "lnexp":
                t = next_tmp()
                act(tmps[t][:], psb[b][:], AF.Ln, reads=[("ps", b), ("eps",)], writes=[("tmp", t)],
                    bias=EPS_AP[0], scale=1.0 / D)
                act(psb[b][:], tmps[t][:], AF.Exp, reads=[("tmp", t)], writes=[("ps", b)], scale=-0.5)
            else:
                t = next_tmp()
                act(tmps[t][:], psb[b][:], AF.Sqrt, reads=[("ps", b), ("eps",)], writes=[("tmp", t)],
                    bias=EPS_AP[0], scale=1.0 / D)
                P.emit("dve", (lambda e: e.reciprocal(out=psb[b][:], in_=tmps[t][:])),
                       reads=[("tmp", t)], writes=[("ps", b)])

        def ss_mm(i, q, first, last):
            b = NPS[i]

            def fn(e):
                return e.matmul(out=psb[b][:], lhsT=ones[:], rhs=sqs[q][:], start=first, stop=last)
            P.emit("pe", fn, reads=[("sq", q), ("ones",)], writes=[("ps", b)])

        def pre_norm_tile(l, gcol, i):
            pend = None
            for k in range(KC):
                q = next_sq()
                act(sqs[q][:], xT[:, k, tsl(i)], AF.Square, reads=[("xT", k, i)], writes=[("sq", q)])
                if pend is not None:
                    ss_mm(i, *pend)
                pend = (q, k == 0, k == KC - 1)
            ss_mm(i, *pend)
            emit_rstd(i)
            b = NPS[i]
            for k in range(KC):
                dve_stt(hT[:, k, tsl(i)], xT[:, k, tsl(i)], smc(l, gcol + k), psb[b][:], ALU.mult, ALU.mult,
                        reads=[("xT", k, i), ("ps", b), ("sm",)], writes=[("hT", k, i)])

        def post_evac(bank, c, i, pend):
            act(houtv[:, c, tsl(i)], psb[bank][:], AF.Copy, reads=[("ps", bank)], writes=[("hout", c, i)])
            q = next_sq()
            act(sqs[q][:], psb[bank][:], AF.Square, reads=[("ps", bank)], writes=[("sq", q)])
            if pend[i] is not None:
                ss_mm(i, *pend[i])
            pend[i] = (q, c == 0, c == KC - 1)

        def tail(l, gidx, pend, i):
            ss_mm(i, *pend[i])
            pend[i] = None
            emit_rstd(i)
            b = NPS[i]
            for c in range(KC):
                t = next_tmp()
                gc = l * 24 + gidx * 8 + c
                dve_stt(tmps[t][:], houtv[:, c, tsl(i)], gp05[:, gc:gc + 1], psb[b][:], ALU.mult, ALU.mult,
                        reads=[("hout", c, i), ("ps", b), ("gp05",)], writes=[("tmp", t)])
                dve_tt(xT[:, c, tsl(i)], xT[:, c, tsl(i)], tmps[t][:], ALU.add,
                       reads=[("xT", c, i), ("tmp", t)], writes=[("xT", c, i)])

        GRP = 3

        def hreads(i):
            return [("hT", k, i) for k in range(KC)]

        def ffn(l, which, hook_i1, next_pre):
            def up_block(s, jb, i):
                sv = slots[s][:, 0:4096].rearrange("p (k n c) -> p k n c", k=KC, n=2)
                for jj in range(2):
                    j = 2 * jb + jj
                    bg, bu = next_ps(), next_ps()
                    mm_group(bg, [(sv[:, k, 0, jj * 128:(jj + 1) * 128], hT[:, k, tsl(i)]) for k in range(KC)],
                             reads=[("w", s)] + hreads(i))
                    mm_group(bu, [(sv[:, k, 1, jj * 128:(jj + 1) * 128], hT[:, k, tsl(i)]) for k in range(KC)],
                             reads=[("w", s)] + hreads(i))
                    t = next_tmp()
                    act(tmps[t][:], psb[bg][:], AF.Silu, reads=[("ps", bg)], writes=[("tmp", t)])
                    dve_tt(arena[:, j, tsl(i)], tmps[t][:], psb[bu][:], ALU.mult,
                           reads=[("tmp", t), ("ps", bu)], writes=[("ar", j, i)])

            gs = [acquire(("gu", l, which, jb)) for jb in range(GRP)]
            for i in range(NT):
                for jb in range(GRP):
                    up_block(gs[jb], jb, i)
                    if i == 0 and jb == 1 and hook_i1 is not None:
                        hook_i1()
            release(GRP)
            for jb in range(GRP, FC // 2):
                s = acquire(("gu", l, which, jb))
                for i in range(NT):
                    up_block(s, jb, i)
                release()
            pend = [None] * NT
            for i in range(NT):
                for c in range(KC):
                    s = acquire(("dn", l, which, c, i))
                    sv = slots[s][:, 0:FC * 128].rearrange("p (k c) -> p k c", k=FC)
                    bo = next_ps()
                    mm_group(bo, [(sv[:, j, :], arena[:, j, tsl(i)]) for j in range(FC)],
                             reads=[("w", s)] + [("ar", j, i) for j in range(FC)])
                    post_evac(bo, c, i, pend)
                    release()
                    if i == NT - 1 and c == 3 and next_pre is not None:
                        ss_mm(i, *pend[i])
                        pend[i] = None
                        next_pre(0)
                tail(l, 0 if which == 1 else 2, pend, i)

        def mixer(l, seq_start, hook_i1, next_pre):
            UA, DM, PG = 0, 8, 16

            def act_copy(out, in_, reads, writes):
                act(out, in_, AF.Copy, reads=reads, writes=writes)

            def m12_block(s, c, i):
                sv = slots[s][:, 0:4096].rearrange("p (k n c) -> p k n c", k=KC, n=4)
                stc = st_cv[:, (l * KC + c) * 2:(l * KC + c) * 2 + 2]
                g = c // 2
                win = POOL_WINDOWS[g]
                stp = st_p[:, (l * KC + c) * 16:(l * KC + c) * 16 + 16]
                bv, bc, bp, bb = next_ps(), next_ps(), next_ps(), next_ps()
                for n_, bk in ((2, bv), (1, bc), (3, bp), (0, bb)):
                    mm_group(bk, [(sv[:, k, n_, :], hT[:, k, tsl(i)]) for k in range(KC)],
                             reads=[("w", s)] + hreads(i))
                tv = next_tmp()
                act(tmps[tv][:], psb[bv][:], AF.Copy, reads=[("ps", bv)], writes=[("tmp", tv)])
                tb = next_tmp()
                act(tmps[tb][:], psb[bb][:], AF.Copy, reads=[("ps", bb)], writes=[("tmp", tb)])
                cv = rot["cv"]
                rot["cv"] = 1 - cv
                cb = cvb[cv]
                act_copy(cb[:, 0:2], stc, reads=[("stcv", l, c)], writes=[("cv", cv)])
                dve_tt(cb[:, 2:514], tmps[tv][:], psb[bc][:], ALU.mult,
                       reads=[("tmp", tv), ("ps", bc)], writes=[("cv", cv)])
                act_copy(stc, cb[:, 512:514], reads=[("cv", cv)], writes=[("stcv", l, c)])
                t0, t1, t2 = next_tmp(), next_tmp(), next_tmp()
                act(tmps[t0][:], cb[:, 0:512], AF.Copy, reads=[("cv", cv), ("sm",)], writes=[("tmp", t0)],
                    scale=smc(l, SM_CW + 0 * 8 + c))
                dve_stt(tmps[t1][:], cb[:, 1:513], smc(l, SM_CW + 1 * 8 + c), tmps[t0][:], ALU.mult, ALU.add,
                        reads=[("cv", cv), ("tmp", t0), ("sm",)], writes=[("tmp", t1)])
                dve_stt(tmps[t2][:], cb[:, 2:514], smc(l, SM_CW + 2 * 8 + c), tmps[t1][:], ALU.mult, ALU.add,
                        reads=[("cv", cv), ("tmp", t1), ("sm",)], writes=[("tmp", t2)])
                dve_tt(arena[:, UA + c, tsl(i)], tmps[t2][:], tmps[tb][:], ALU.mult,
                       reads=[("tmp", t2), ("tmp", tb)], writes=[("ar", UA + c, i)])
                pi = rot["pb"]
                rot["pb"] = 1 - pi
                pbuf, sA, sB = pbs[pi]
                act_copy(pbuf[:, 0:16], stp, reads=[("stp", l, c)], writes=[("pb", pi, 0)])
                act(pbuf[:, 16:528], psb[bp][:], AF.Copy, reads=[("ps", bp)], writes=[("pb", pi, 0)])
                act_copy(stp, pbuf[:, 512:528], reads=[("pb", pi, 0)], writes=[("stp", l, c)])
                src, dst, skey, dkey = pbuf, sA, ("pb", pi, 0), ("pb", pi, 1)
                step, lo = 1, 1
                while step < win:
                    dve_tt(dst[:, lo:528], src[:, lo:528], src[:, lo - step:528 - step], ALU.add,
                           reads=[skey], writes=[dkey])
                    step *= 2
                    lo = 2 * step - 1
                    if src is pbuf:
                        src, skey = sA, ("pb", pi, 1)
                        dst, dkey = sB, ("pb", pi, 2)
                    else:
                        src, dst = dst, src
                        skey, dkey = dkey, skey
                fin, fkey = src, skey
                dve_stt(arena[:, DM + c, tsl(i)], fin[:, 16:528], 1.0 / win, pbuf[:, 16:528],
                        ALU.mult, ALU.subtract, reads=[fkey, ("pb", pi, 0)], writes=[("ar", DM + c, i)])
                if seq_start and i == 0:
                    t = next_tmp()
                    dve_tt(tmps[t][:, 0:16], fin[:, 16:32], invc[:, g * 16:(g + 1) * 16], ALU.mult,
                           reads=[fkey, ("invc",)], writes=[("tmp", t)])
                    dve_tt(arena[:, DM + c, 0:16], tmps[t][:, 0:16], pbuf[:, 16:32], ALU.subtract,
                           reads=[("tmp", t), ("pb", pi, 0)], writes=[("ar", DM + c, i)])

            gs = [acquire(("m12", l, c)) for c in range(GRP)]
            for i in range(NT):
                for c in range(GRP):
                    m12_block(gs[c], c, i)
                    if i == 0 and c == 1 and hook_i1 is not None:
                        hook_i1()
            release(GRP)
            for c in range(GRP, KC):
                s = acquire(("m12", l, c))
                for i in range(NT):
                    m12_block(s, c, i)
                release()
            s = acquire(("m3", l))
            sv = slots[s][:, 0:2048].rearrange("p (k c) -> p k c", k=8)
            for g in range(4):
                for jj in range(2):
                    c = 2 * g + jj
                    for i in range(NT):
                        bk = next_ps()
                        mm_group(bk, [(sv[:, 2 * g + kk, jj * 128:(jj + 1) * 128], arena[:, DM + 2 * g + kk, tsl(i)])
                                      for kk in range(2)],
                                 reads=[("w", s)] + [("ar", DM + 2 * g + kk, i) for kk in range(2)])
                        act(arena[:, PG + c, tsl(i)], psb[bk][:], AF.Identity,
                            reads=[("ps", bk), ("bps",), ("sm",)], writes=[("ar", PG + c, i)],
                            bias=bps[:, l * 8 + c:l * 8 + c + 1], scale=smc(l, SM_PS + c))
            release()
            for c in range(KC):
                s = acquire(("m4", l, c))
                sv = slots[s][:, 0:4096].rearrange("p (k n c) -> p k n c", k=KC, n=4)
                for i in range(NT):
                    res = []
                    for br, (wsel, gsel, src_off, bcol) in enumerate(((0, 2, UA, SM_BG + c), (1, 3, PG, SM_BG + 8 + c))):
                        by, bgt = next_ps(), next_ps()
                        mm_group(by, [(sv[:, k, wsel, :], arena[:, src_off + k, tsl(i)]) for k in range(KC)],
                                 reads=[("w", s)] + [("ar", src_off + k, i) for k in range(KC)])
                        mm_group(bgt, [(sv[:, k, gsel, :], hT[:, k, tsl(i)]) for k in range(KC)],
                                 reads=[("w", s)] + hreads(i))
                        tg, ty = next_tmp(), next_tmp()
                        act(tmps[tg][:], psb[bgt][:], AF.Sigmoid, reads=[("ps", bgt), ("sm",)], writes=[("tmp", tg)],
                            bias=smc(l, bcol))
                        dve_tt(tmps[ty][:], tmps[tg][:], psb[by][:], ALU.mult,
                               reads=[("tmp", tg), ("ps", by)], writes=[("tmp", ty)])
                        res.append(ty)
                    dve_tt(arena[:, DM + c, tsl(i)], tmps[res[0]][:], tmps[res[1]][:], ALU.add,
                           reads=[("tmp", res[0]), ("tmp", res[1])], writes=[("ar", DM + c, i)])
                release()
            pend = [None] * NT
            for i in range(NT):
                for ob in range(2):
                    s = acquire(("m5", l, ob, i))
                    sv = slots[s][:, 0:4096].rearrange("p (k c) -> p k c", k=KC)
                    for cc in range(4):
                        c = 4 * ob + cc
                        bo = next_ps()
                        mm_group(bo, [(sv[:, k, cc * 128:(cc + 1) * 128], arena[:, DM + k, tsl(i)]) for k in range(KC)],
                                 reads=[("w", s)] + [("ar", DM + k, i) for k in range(KC)])
                        post_evac(bo, c, i, pend)
                    release()
                    if i == NT - 1 and ob == 0 and next_pre is not None:
                        ss_mm(i, *pend[i])
                        pend[i] = None
                        next_pre(0)
                tail(l, 1, pend, i)

        P.emit("sp", (lambda e: e.dma_start(out=sm[:], in_=sm_d[:, :])), writes=[("sm",)], stream="cst")
        P.emit("sp", (lambda e: e.dma_start(out=ident[:], in_=id_d[:, :])), writes=[("ident",)], stream="cst2")
        P.emit("dve", (lambda e: e.memset(ones[:], 1.0)), writes=[("ones",)])
        for g, win in enumerate(POOL_WINDOWS):
            P.emit("dve", (lambda e, g=g, win=win: e.memset(invc[:, g * 16:(g + 1) * 16], 1.0 / win)), writes=[("invc",)])
            for t_ in range(win - 1):
                P.emit("dve", (lambda e, g=g, t_=t_: e.memset(invc[:, g * 16 + t_:g * 16 + t_ + 1], 1.0 / (t_ + 1))),
                       writes=[("invc",)])
        eps_t = sb("eps_t", [128, 1], F32)
        EPS_AP = [eps_t[:, 0:1]]
        P.emit("dve", (lambda e: e.memset(eps_t[:], EPS)), writes=[("eps",)])
        for l in range(depth):
            for gidx, (col, fac) in enumerate(((SM_F1POST, 0.5), (SM_MPOST, 1.0), (SM_F2POST, 0.5))):
                o = l * 24 + gidx * 8
                c0 = l * SM_PER_LAYER + col
                P.emit("dve", (lambda e, o=o, c0=c0, fac=fac: e.tensor_scalar(
                    out=gp05[:, o:o + 8], in0=sm[:, c0:c0 + 8], scalar1=fac, scalar2=None, op0=ALU.mult)),
                    reads=[("sm",)], writes=[("gp05",)])
            c0 = l * SM_PER_LAYER
            P.emit("dve", (lambda e, l=l, c0=c0: e.tensor_tensor(
                out=bps[:, l * 8:l * 8 + 8], in0=sm[:, c0 + SM_BPG:c0 + SM_BPG + 8],
                in1=sm[:, c0 + SM_PS:c0 + SM_PS + 8], op=ALU.mult)), reads=[("sm",)], writes=[("bps",)])

        prefetch()
        tiles_per_seq = seq // tt
        assert NT == 2

        def xload(ti, i):
            tok = ti * tt + i * 512
            for h in range(2):
                src = x_d[tok:tok + 512, h * 512:(h + 1) * 512].rearrange("(s p) t -> p s t", p=128)
                dst = houtv[:, :, tsl(i)].rearrange("p (s h) t -> p s h t", h=2)[:, :, h, :]
                P.emit("sp", (lambda e, dst=dst, src=src: e.dma_start(out=dst, in_=src)),
                       writes=[("hout", c, i) for c in range(KC)], stream=f"xin{i}")

        def in_transposes(i):
            for k in range(KC):
                h, t0 = k // 4, (k % 4) * 128
                bk = next_ps()

                def fn(e, k=k, h=h, t0=t0, bk=bk):
                    inst = None
                    for s4 in range(4):
                        inst = e.transpose(out=psb[bk][:, s4 * 128:(s4 + 1) * 128],
                                           in_=houtv[:, 2 * s4 + h, i * 512 + t0:i * 512 + t0 + 128],
                                           identity=ident[:])
                    return inst
                P.emit("pe", fn, reads=[("hout", c, i) for c in range(KC)] + [("ident",)], writes=[("ps", bk)])
                if k % 2 == 0:
                    act(xT[:, k, tsl(i)], psb[bk][:], AF.Copy, reads=[("ps", bk)], writes=[("xT", k, i)])
                else:
                    dve_copy(xT[:, k, tsl(i)], psb[bk][:], reads=[("ps", bk)], writes=[("xT", k, i)])

        def output_half(ti, i):
            for s4 in range(4):
                for h in range(2):
                    bk = next_ps()

                    def fn(e, s4=s4, h=h, bk=bk):
                        inst = None
                        for k4 in range(4):
                            k = h * 4 + k4
                            inst = e.transpose(out=psb[bk][:, k4 * 128:(k4 + 1) * 128],
                                               in_=xT[:, k, i * 512 + s4 * 128:i * 512 + (s4 + 1) * 128],
                                               identity=ident[:])
                        return inst
                    P.emit("pe", fn, reads=[("xT", h * 4 + k4, i) for k4 in range(4)] + [("ident",)],
                           writes=[("ps", bk)])
                    t = next_tmp()
                    if (s4 + h) % 2 == 0:
                        act(tmps[t][:], psb[bk][:], AF.Copy, reads=[("ps", bk)], writes=[("tmp", t)])
                    else:
                        dve_copy(tmps[t][:], psb[bk][:], reads=[("ps", bk)], writes=[("tmp", t)])
                    r0 = ti * tt + i * 512 + s4 * 128
                    P.emit("sp", (lambda e, t=t, r0=r0, h=h: e.dma_start(
                        out=out_d[r0:r0 + 128, h * 512:(h + 1) * 512], in_=tmps[t][:])),
                        reads=[("tmp", t)], stream=f"xo{t}")

        subs = []
        for l in range(depth):
            subs += [("ffn", l, 1), ("mix", l, 0), ("ffn", l, 2)]

        def pre_fn(sub):
            kind, l_, which = sub
            gcol = SM_MPRE if kind == "mix" else (SM_F1PRE if which == 1 else SM_F2PRE)
            return lambda i, l_=l_, gcol=gcol: pre_norm_tile(l_, gcol, i)

        for ti in range(n_tt):
            seq_start = (ti % tiles_per_seq == 0)
            if ti == 0:
                xload(0, 0)
                xload(0, 1)
                in_transposes(0)
                pre_fn(subs[0])(0)
            if seq_start:
                P.emit("dve", (lambda e: e.memset(st_cv[:], 0.0)),
                       writes=[("stcv", l, c) for l in range(depth) for c in range(KC)])
                P.emit("dve", (lambda e: e.memset(st_p[:], 0.0)),
                       writes=[("stp", l, c) for l in range(depth) for c in range(KC)])
            for si, sub in enumerate(subs):
                mypre = pre_fn(sub)
                first, last = (si == 0), (si == len(subs) - 1)

                def hook(mypre=mypre, first=first, ti=ti):
                    if first:
                        if ti > 0:
                            xload(ti, 1)
                            output_half(ti - 1, 1)
                        in_transposes(1)
                    mypre(1)

                if not last:
                    nxt = pre_fn(subs[si + 1])
                else:
                    def nxt(i, ti=ti):
                        assert i == 0
                        if ti + 1 < n_tt:
                            xload(ti + 1, 0)
                        output_half(ti, 0)
                        if ti + 1 < n_tt:
                            in_transposes(0)
                            pre_fn(subs[0])(0)
                if sub[0] == "ffn":
                    ffn(sub[1], sub[2], hook, nxt)
                else:
                    mixer(sub[1], seq_start, hook, nxt)
        output_half(n_tt - 1, 1)
        P.emit("sp", None, reads=[], writes=[("tmp", t) for t in range(ntmp)])
        assert wstate["cur"] == len(sched)

        with nc.Block() as block:
            @block.tensor
            def _(e):
                P.replay("pe", e, sems)

            @block.scalar
            def _(e):
                P.replay("act", e, sems)

            @block.vector
            def _(e):
                P.replay("dve", e, sems)

            @block.gpsimd
            def _(e):
                P.replay("pool", e, sems)

            @block.sync
            def _(e):
                P.replay("sp", e, sems)
    return nc


def pack_smalls(inp, depth):
    sm = np.zeros((128, SM_PER_LAYER * depth), np.float32)

    def put(l, col, vec):
        v = np.asarray(vec, np.float32).reshape(-1, 128)
        sm[:, l * SM_PER_LAYER + col:l * SM_PER_LAYER + col + v.shape[0]] = v.T

    for l in range(depth):
        put(l, SM_F1PRE, inp["ffn1_pre"][l])
        put(l, SM_F1POST, inp["ffn1_post"][l])
        put(l, SM_MPRE, inp["mix_pre"][l])
        put(l, SM_MPOST, inp["mix_post"][l])
        put(l, SM_F2PRE, inp["ffn2_pre"][l])
        put(l, SM_F2POST, inp["ffn2_post"][l])
        put(l, SM_BG, inp["b_gate"][l])
        put(l, SM_CW, inp["conv_w"][l])
        put(l, SM_BPG, inp["b_pool_group"][l])
        put(l, SM_PS, inp["pool_scale"][l])
    return sm


def make_in_maps(inp, depth, ncores, xs):
    common = {"smalls": pack_smalls(inp, depth), "ident": np.eye(128, dtype=np.float32)}
    for l in range(depth):
        def pack_gu(w):
            w = np.asarray(w, np.float32)
            blocks = []
            for jb in range(FC // 2):
                blocks += [w[:, jb * 256:(jb + 1) * 256], w[:, DFF + jb * 256:DFF + (jb + 1) * 256]]
            return np.ascontiguousarray(np.concatenate(blocks, axis=1))

        win = np.asarray(inp["w_in"][l], np.float32)
        wco = np.asarray(inp["w_conv_out"][l], np.float32)
        wpo = np.asarray(inp["w_pool_out"][l], np.float32)
        m12, m4 = [], []
        for c in range(KC):
            cs = slice(c * 128, (c + 1) * 128)
            m12 += [win[:, s_ * D + c * 128:s_ * D + (c + 1) * 128] for s_ in range(4)]
            m4 += [wco[:, cs], wpo[:, cs], win[:, 4 * D + c * 128:4 * D + (c + 1) * 128],
                   win[:, 5 * D + c * 128:5 * D + (c + 1) * 128]]
        common[f"f1gu{l}"] = pack_gu(inp["ffn1_w_gate_up"][l])
        common[f"f1d{l}"] = np.ascontiguousarray(inp["ffn1_w_down"][l], np.float32)
        common[f"m12w{l}"] = np.ascontiguousarray(np.concatenate(m12, axis=1))
        common[f"m4w{l}"] = np.ascontiguousarray(np.concatenate(m4, axis=1))
        common[f"wpg{l}"] = np.ascontiguousarray(inp["w_pool_group"][l], np.float32).reshape(4 * 256, 256)
        common[f"wo{l}"] = np.ascontiguousarray(inp["w_o"][l], np.float32)
        common[f"f2gu{l}"] = pack_gu(inp["ffn2_w_gate_up"][l])
        common[f"f2d{l}"] = np.ascontiguousarray(inp["ffn2_w_down"][l], np.float32)
    maps = []
    for c in range(ncores):
        m = dict(common)
        m["x"] = xs[c]
        maps.append(m)
    return maps


_NC_CACHE = {}


def kernel(**inputs):
    inp = {k: np.asarray(v) for k, v in inputs.items()}
    x = inp["x"].astype(np.float32, copy=False)
    B, S, _ = x.shape
    depth = inp["ffn1_pre"].shape[0]
    nseq = B // NCORES
    xs = [np.ascontiguousarray(x[c * nseq:(c + 1) * nseq].reshape(nseq * S, D)) for c in range(NCORES)]
    key = (nseq, S, depth)
    if key not in _NC_CACHE:
        _NC_CACHE[key] = build_program(nseq, S, depth)
    nc = _NC_CACHE[key]
    res = run_bass_kernel_spmd(nc, make_in_maps(inp, depth, NCORES, xs), core_ids=list(range(NCORES)))
    out = np.stack([r["out"].reshape(nseq, S, D) for r in res.results], axis=0).reshape(B, S, D)
    return out.astype(np.float32, copy=False)
```

```python
import bisect
from contextlib import ExitStack

import numpy as np
import concourse.bass as bass
import concourse.mybir as mybir
from concourse.bass_utils import run_bass_kernel_spmd

F32 = mybir.dt.float32
BF16 = mybir.dt.bfloat16
AF = mybir.ActivationFunctionType
ALU = mybir.AluOpType

D = 1024
KC = 8
DFF = 2816
FC = 22
EPS = 1e-6
NCORES = 8
POOL_WINDOWS = (2, 4, 8, 16)
RSTD_MODE = "lnexp"

ENGS = ("pe", "act", "dve", "pool", "sp")

SM_F1PRE, SM_F1POST, SM_MPRE, SM_MPOST, SM_F2PRE, SM_F2POST = 0, 8, 16, 24, 32, 40
SM_BG = 48
SM_CW = 64
SM_BPG = 88
SM_PS = 96
SM_PER_LAYER = 104


class Prog:
    def __init__(self):
        self.ins = {e: [] for e in ENGS}
        self.res = {}
        self.seen = {e: {} for e in ENGS}
        self.scount = {}
        self.milestones = {e: set() for e in ENGS}

    def emit(self, eng, fn, reads=(), writes=(), stream=None):
        idx = len(self.ins[eng])
        if stream is None:
            me = (eng, idx)
        else:
            sidx = self.scount.get(stream, 0)
            self.scount[stream] = sidx + 1
            me = (stream, sidx)
        deps = {}

        def add(dep, kind):
            if dep is None:
                return
            src, i = dep
            if stream is None and src == eng:
                if eng == "pe":
                    return
            if i > deps.get(src, -1):
                deps[src] = i

        for r in reads:
            st = self.res.get(r)
            if st is not None:
                add(st["w"], "raw")
        for w in writes:
            st = self.res.get(w)
            if st is not None:
                add(st["w"], "waw")
                for rd in st["r"]:
                    add(rd, "war")
        waits = []
        for src, i in deps.items():
            if i > self.seen[eng].get(src, -1):
                self.seen[eng][src] = i
                waits.append((src, i))
                if src in self.milestones:
                    self.milestones[src].add(i)
        for r in reads:
            st = self.res.setdefault(r, {"w": None, "r": []})
            st["r"].append(me)
        for w in writes:
            self.res[w] = {"w": me, "r": []}
        self.ins[eng].append((waits, fn, stream, idx))
        return me

    def replay(self, eng, e, sems):
        ms = sorted(self.milestones[eng])
        msorted = {s: sorted(self.milestones[s]) for s in self.milestones}

        def semval(src, i):
            if src in msorted:
                return bisect.bisect_right(msorted[src], i)
            return 16 * (i + 1)

        msset = set(ms)
        for waits, fn, stream, idx in self.ins[eng]:
            for src, i in waits:
                e.wait_ge(sems[src], semval(src, i))
            if fn is None:
                continue
            inst = fn(e)
            if stream is not None:
                inst.then_inc(sems[stream], 16)
            elif idx in msset:
                inst.then_inc(sems[eng], 1)


def build_program(nseq, seq, depth, tt=1024, nslots=4, ntmp=11):
    NT = tt // 512
    assert tt % 512 == 0 and seq % tt == 0
    ntok = nseq * seq
    nsub = tt // 128
    nc = bass.Bass("TRN2", target_bir_lowering=False)

    x_d = nc.dram_tensor("x", [ntok, D], F32, kind="ExternalInput").ap()
    out_d = nc.dram_tensor("out", [ntok, D], F32, kind="ExternalOutput").ap()
    sm_d = nc.dram_tensor("smalls", [128, SM_PER_LAYER * depth], F32, kind="ExternalInput").ap()
    id_d = nc.dram_tensor("ident", [128, 128], F32, kind="ExternalInput").ap()
    W = []
    for l in range(depth):
        d = {}
        d["f1gu"] = nc.dram_tensor(f"f1gu{l}", [D, 2 * DFF], F32, kind="ExternalInput").ap()
        d["f1d"] = nc.dram_tensor(f"f1d{l}", [DFF, D], F32, kind="ExternalInput").ap()
        d["m12w"] = nc.dram_tensor(f"m12w{l}", [D, 4 * D], F32, kind="ExternalInput").ap()
        d["m4w"] = nc.dram_tensor(f"m4w{l}", [D, 4 * D], F32, kind="ExternalInput").ap()
        d["wpg"] = nc.dram_tensor(f"wpg{l}", [4 * 256, 256], F32, kind="ExternalInput").ap()
        d["wo"] = nc.dram_tensor(f"wo{l}", [D, D], F32, kind="ExternalInput").ap()
        d["f2gu"] = nc.dram_tensor(f"f2gu{l}", [D, 2 * DFF], F32, kind="ExternalInput").ap()
        d["f2d"] = nc.dram_tensor(f"f2d{l}", [DFF, D], F32, kind="ExternalInput").ap()
        W.append(d)

    def kview(ap2d, c0, c1):
        return ap2d.rearrange("(k p) c -> p k c", p=128)[:, :, c0:c1]

    P = Prog()
    es = ExitStack()
    with es:
        def sb(name, shape, dt):
            return es.enter_context(nc.sbuf_tensor("sb_" + name, shape, dt))

        xT = sb("xT", [128, KC, tt], F32)
        hT = sb("hT", [128, KC, tt], BF16)
        arena = sb("arena", [128, 24, tt], BF16)
        hout = sb("hout", [128, KC * tt], F32)
        houtv = hout[:].rearrange("p (c t) -> p c t", t=tt)
        stagev = hout[:].rearrange("p (s d) -> p s d", d=D)
        tmps = [sb(f"tmp{i}", [128, 512], F32) for i in range(ntmp)]
        sqs = [sb(f"sq{i}", [128, 512], BF16) for i in range(4)]
        cvb = [sb(f"cv{i}", [128, 2 + 512], F32) for i in range(2)]
        pbs = [[sb(f"pb{i}_{j}", [128, 16 + 512], F32) for j in range(3)] for i in range(2)]
        slots = [sb(f"ws{i}", [128, 4096], BF16) for i in range(nslots)]
        ones = sb("ones", [128, 128], BF16)
        ident = sb("ident", [128, 128], F32)
        sm = sb("sm", [128, SM_PER_LAYER * depth], F32)
        gp05 = sb("gp05", [128, depth * 24], F32)
        bps = sb("bps", [128, depth * 8], F32)
        invc = sb("invc", [128, 4 * 16], F32)
        st_cv = sb("st_cv", [128, depth * KC * 2], F32)
        st_p = sb("st_p", [128, depth * KC * 16], F32)
        psb = [es.enter_context(nc.psum_tensor(f"ps{i}", [128, 512], F32)) for i in range(8)]

        sem_names = list(ENGS) + [f"w{i}" for i in range(nslots)] + ["xin0", "xin1", "cst", "cst2"] + [f"xo{i}" for i in range(ntmp)]
        sems = {n: es.enter_context(nc.semaphore(f"s_{n}")) for n in sem_names}

        rot = {"ps": 0, "tmp": 0, "sq": 0, "cv": 0, "pb": 0}

        def next_ps():
            i = rot["ps"]
            rot["ps"] = (i + 1) % 6
            return i

        def next_tmp():
            i = rot["tmp"]
            rot["tmp"] = (i + 1) % ntmp
            return i

        def next_sq():
            i = rot["sq"]
            rot["sq"] = (i + 1) % 4
            return i

        def tsl(i):
            return slice(i * 512, (i + 1) * 512)

        sched = []

        def flat(c0, c1):
            return lambda s: s[:, c0:c1].rearrange("p (k c) -> p k c", k=KC)

        def sub3(n, j):
            return lambda s: s[:, 0:KC * n * 128].rearrange("p (k n c) -> p k n c", k=KC, n=n)[:, :, j, :]

        def ffn_blocks(l, which):
            gu = W[l]["f1gu" if which == 1 else "f2gu"]
            dn = W[l]["f1d" if which == 1 else "f2d"]
            for jb in range(FC // 2):
                sched.append((("gu", l, which, jb), [
                    (flat(0, 4096), kview(gu, jb * 512, jb * 512 + 512))]))
            for i in range(NT):
                for c in range(KC):
                    sched.append((("dn", l, which, c, i), [
                        (lambda s: s[:, 0:FC * 128].rearrange("p (k c) -> p k c", k=FC),
                         kview(dn, c * 128, c * 128 + 128)),
                    ]))

        def mixer_blocks(l):
            for c in range(KC):
                sched.append((("m12", l, c), [
                    (flat(0, 4096), kview(W[l]["m12w"], c * 512, c * 512 + 512))]))
            sched.append((("m3", l), [
                (lambda s: s[:, 0:2048].rearrange("p (k c) -> p k c", k=8), kview(W[l]["wpg"], 0, 256))]))
            for c in range(KC):
                sched.append((("m4", l, c), [
                    (flat(0, 4096), kview(W[l]["m4w"], c * 512, c * 512 + 512))]))
            for i in range(NT):
                for ob in range(2):
                    sched.append((("m5", l, ob, i), [
                        (flat(0, 4096), kview(W[l]["wo"], ob * 512, ob * 512 + 512))]))

        n_tt = ntok // tt
        for _t in range(n_tt):
            for l in range(depth):
                ffn_blocks(l, 1)
                mixer_blocks(l)
                ffn_blocks(l, 2)

        wstate = {"cur": 0, "emitted": 0, "released": 0}

        def emit_wdma(m):
            tag, dmas = sched[m]
            s = m % nslots
            for dst_fn, src in dmas:
                dst = dst_fn(slots[s])
                P.emit("pool", (lambda e, dst=dst, src=src: e.dma_start(out=dst, in_=src)),
                       writes=[("w", s)], stream=f"w{s}")

        def prefetch():
            upto = min(wstate["released"] + nslots, len(sched))
            for m in range(wstate["emitted"], upto):
                emit_wdma(m)
            wstate["emitted"] = max(wstate["emitted"], upto)

        def acquire(tag):
            n = wstate["cur"]
            assert sched[n][0] == tag, (sched[n][0], tag)
            assert n < wstate["emitted"], "weight block not prefetched"
            wstate["cur"] = n + 1
            return n % nslots

        def release(count=1):
            wstate["released"] += count
            assert wstate["released"] <= wstate["cur"]
            prefetch()

        def mm_group(bank, pairs, reads):
            n = len(pairs)

            def fn(e, pairs=pairs, bank=bank, n=n):
                inst = None
                for q, (l_, r_) in enumerate(pairs):
                    inst = e.matmul(out=psb[bank][:], lhsT=l_, rhs=r_, start=(q == 0), stop=(q == n - 1))
                return inst
            P.emit("pe", fn, reads=reads, writes=[("ps", bank)])

        def act(out, in_, func, reads, writes, bias=None, scale=None):
            kw = {}
            if bias is not None:
                kw["bias"] = bias
            if scale is not None:
                kw["scale"] = scale
            P.emit("act", (lambda e: e.activation(out=out, in_=in_, func=func, **kw)), reads=reads, writes=writes)

        def dve_tt(out, in0, in1, op, reads, writes):
            P.emit("dve", (lambda e: e.tensor_tensor(out=out, in0=in0, in1=in1, op=op)), reads=reads, writes=writes)

        def dve_stt(out, in0, scalar, in1, op0, op1, reads, writes):
            P.emit("dve", (lambda e: e.scalar_tensor_tensor(out=out, in0=in0, scalar=scalar, in1=in1,
                                                            op0=op0, op1=op1)), reads=reads, writes=writes)

        def dve_copy(out, in_, reads, writes):
            P.emit("dve", (lambda e: e.tensor_copy(out=out, in_=in_)), reads=reads, writes=writes)

        def smc(l, col):
            c = l * SM_PER_LAYER + col
            return sm[:, c:c + 1]

        NPS = [6, 7]

        def emit_rstd(i):
            b = NPS[i]
            if RSTD_MODE == "lnexp":
                t = next_tmp()
                act(tmps[t][:], psb[b][:], AF.Ln, reads=[("ps", b), ("eps",)], writes=[("tmp", t)],
                    bias=EPS_AP[0], scale=1.0 / D)
                act(psb[b][:], tmps[t][:], AF.Exp, reads=[("tmp", t)], writes=[("ps", b)], scale=-0.5)
            else:
                t = next_tmp()
                act(tmps[t][:], psb[b][:], AF.Sqrt, reads=[("ps", b), ("eps",)], writes=[("tmp", t)],
                    bias=EPS_AP[0], scale=1.0 / D)
                P.emit("dve", (lambda e: e.reciprocal(out=psb[b][:], in_=tmps[t][:])),
                       reads=[("tmp", t)], writes=[("ps", b)])

        def ss_mm(i, q, first, last):
            b = NPS[i]

            def fn(e):
                return e.matmul(out=psb[b][:], lhsT=ones[:], rhs=sqs[q][:], start=first, stop=last)
            P.emit("pe", fn, reads=[("sq", q), ("ones",)], writes=[("ps", b)])

        def pre_norm_tile(l, gcol, i):
            pend = None
            for k in range(KC):
                q = next_sq()
                act(sqs[q][:], xT[:, k, tsl(i)], AF.Square, reads=[("xT", k, i)], writes=[("sq", q)])
                if pend is not None:
                    ss_mm(i, *pend)
                pend = (q, k == 0, k == KC - 1)
            ss_mm(i, *pend)
            emit_rstd(i)
            b = NPS[i]
            for k in range(KC):
                dve_stt(hT[:, k, tsl(i)], xT[:, k, tsl(i)], smc(l, gcol + k), psb[b][:], ALU.mult, ALU.mult,
                        reads=[("xT", k, i), ("ps", b), ("sm",)], writes=[("hT", k, i)])

        def post_evac(bank, c, i, pend):
            act(houtv[:, c, tsl(i)], psb[bank][:], AF.Copy, reads=[("ps", bank)], writes=[("hout", c, i)])
            q = next_sq()
            act(sqs[q][:], psb[bank][:], AF.Square, reads=[("ps", bank)], writes=[("sq", q)])
            if pend[i] is not None:
                ss_mm(i, *pend[i])
            pend[i] = (q, c == 0, c == KC - 1)

        def tail(l, gidx, pend, i):
            ss_mm(i, *pend[i])
            pend[i] = None
            emit_rstd(i)
            b = NPS[i]
            for c in range(KC):
                t = next_tmp()
                gc = l * 24 + gidx * 8 + c
                dve_stt(tmps[t][:], houtv[:, c, tsl(i)], gp05[:, gc:gc + 1], psb[b][:], ALU.mult, ALU.mult,
                        reads=[("hout", c, i), ("ps", b), ("gp05",)], writes=[("tmp", t)])
                dve_tt(xT[:, c, tsl(i)], xT[:, c, tsl(i)], tmps[t][:], ALU.add,
                       reads=[("xT", c, i), ("tmp", t)], writes=[("xT", c, i)])

        GRP = 3

        def hreads(i):
            return [("hT", k, i) for k in range(KC)]

        def ffn(l, which, hook_i1, next_pre):
            def up_block(s, jb, i):
                sv = slots[s][:, 0:4096].rearrange("p (k n c) -> p k n c", k=KC, n=2)
                for jj in range(2):
                    j = 2 * jb + jj
                    bg, bu = next_ps(), next_ps()
                    mm_group(bg, [(sv[:, k, 0, jj * 128:(jj + 1) * 128], hT[:, k, tsl(i)]) for k in range(KC)],
                             reads=[("w", s)] + hreads(i))
                    mm_group(bu, [(sv[:, k, 1, jj * 128:(jj + 1) * 128], hT[:, k, tsl(i)]) for k in range(KC)],
                             reads=[("w", s)] + hreads(i))
                    t = next_tmp()
                    act(tmps[t][:], psb[bg][:], AF.Silu, reads=[("ps", bg)], writes=[("tmp", t)])
                    dve_tt(arena[:, j, tsl(i)], tmps[t][:], psb[bu][:], ALU.mult,
                           reads=[("tmp", t), ("ps", bu)], writes=[("ar", j, i)])

            gs = [acquire(("gu", l, which, jb)) for jb in range(GRP)]
            for i in range(NT):
                for jb in range(GRP):
                    up_block(gs[jb], jb, i)
                    if i == 0 and jb == 1 and hook_i1 is not None:
                        hook_i1()
            release(GRP)
            for jb in range(GRP, FC // 2):
                s = acquire(("gu", l, which, jb))
                for i in range(NT):
                    up_block(s, jb, i)
                release()
            pend = [None] * NT
            for i in range(NT):
                for c in range(KC):
                    s = acquire(("dn", l, which, c, i))
                    sv = slots[s][:, 0:FC * 128].rearrange("p (k c) -> p k c", k=FC)
                    bo = next_ps()
                    mm_group(bo, [(sv[:, j, :], arena[:, j, tsl(i)]) for j in range(FC)],
                             reads=[("w", s)] + [("ar", j, i) for j in range(FC)])
                    post_evac(bo, c, i, pend)
                    release()
                    if i == NT - 1 and c == 3 and next_pre is not None:
                        ss_mm(i, *pend[i])
                        pend[i] = None
                        next_pre(0)
                tail(l, 0 if which == 1 else 2, pend, i)

        def mixer(l, seq_start, hook_i1, next_pre):
            UA, DM, PG = 0, 8, 16

            def act_copy(out, in_, reads, writes):
                act(out, in_, AF.Copy, reads=reads, writes=writes)

            def m12_block(s, c, i):
                sv = slots[s][:, 0:4096].rearrange("p (k n c) -> p k n c", k=KC, n=4)
                stc = st_cv[:, (l * KC + c) * 2:(l * KC + c) * 2 + 2]
                g = c // 2
                win = POOL_WINDOWS[g]
                stp = st_p[:, (l * KC + c) * 16:(l * KC + c) * 16 + 16]
                bv, bc, bp, bb = next_ps(), next_ps(), next_ps(), next_ps()
                for n_, bk in ((2, bv), (1, bc), (3, bp), (0, bb)):
                    mm_group(bk, [(sv[:, k, n_, :], hT[:, k, tsl(i)]) for k in range(KC)],
                             reads=[("w", s)] + hreads(i))
                tv = next_tmp()
                act(tmps[tv][:], psb[bv][:], AF.Copy, reads=[("ps", bv)], writes=[("tmp", tv)])
                tb = next_tmp()
                act(tmps[tb][:], psb[bb][:], AF.Copy, reads=[("ps", bb)], writes=[("tmp", tb)])
                cv = rot["cv"]
                rot["cv"] = 1 - cv
                cb = cvb[cv]
                act_copy(cb[:, 0:2], stc, reads=[("stcv", l, c)], writes=[("cv", cv)])
                dve_tt(cb[:, 2:514], tmps[tv][:], psb[bc][:], ALU.mult,
                       reads=[("tmp", tv), ("ps", bc)], writes=[("cv", cv)])
                act_copy(stc, cb[:, 512:514], reads=[("cv", cv)], writes=[("stcv", l, c)])
                t0, t1, t2 = next_tmp(), next_tmp(), next_tmp()
                act(tmps[t0][:], cb[:, 0:512], AF.Copy, reads=[("cv", cv), ("sm",)], writes=[("tmp", t0)],
                    scale=smc(l, SM_CW + 0 * 8 + c))
                dve_stt(tmps[t1][:], cb[:, 1:513], smc(l, SM_CW + 1 * 8 + c), tmps[t0][:], ALU.mult, ALU.add,
                        reads=[("cv", cv), ("tmp", t0), ("sm",)], writes=[("tmp", t1)])
                dve_stt(tmps[t2][:], cb[:, 2:514], smc(l, SM_CW + 2 * 8 + c), tmps[t1][:], ALU.mult, ALU.add,
                        reads=[("cv", cv), ("tmp", t1), ("sm",)], writes=[("tmp", t2)])
                dve_tt(arena[:, UA + c, tsl(i)], tmps[t2][:], tmps[tb][:], ALU.mult,
                       reads=[("tmp", t2), ("tmp", tb)], writes=[("ar", UA + c, i)])
                pi = rot["pb"]
                rot["pb"] = 1 - pi
                pbuf, sA, sB = pbs[pi]
                act_copy(pbuf[:, 0:16], stp, reads=[("stp", l, c)], writes=[("pb", pi, 0)])
                act(pbuf[:, 16:528], psb[bp][:], AF.Copy, reads=[("ps", bp)], writes=[("pb", pi, 0)])
                act_copy(stp, pbuf[:, 512:528], reads=[("pb", pi, 0)], writes=[("stp", l, c)])
                src, dst, skey, dkey = pbuf, sA, ("pb", pi, 0), ("pb", pi, 1)
                step, lo = 1, 1
                while step < win:
                    dve_tt(dst[:, lo:528], src[:, lo:528], src[:, lo - step:528 - step], ALU.add,
                           reads=[skey], writes=[dkey])
                    step *= 2
                    lo = 2 * step - 1
                    if src is pbuf:
                        src, skey = sA, ("pb", pi, 1)
                        dst, dkey = sB, ("pb", pi, 2)
                    else:
                        src, dst = dst, src
                        skey, dkey = dkey, skey
                fin, fkey = src, skey
                dve_stt(arena[:, DM + c, tsl(i)], fin[:, 16:528], 1.0 / win, pbuf[:, 16:528],
                        ALU.mult, ALU.subtract, reads=[fkey, ("pb", pi, 0)], writes=[("ar", DM + c, i)])
                if seq_start and i == 0:
                    t = next_tmp()
                    dve_tt(tmps[t][:, 0:16], fin[:, 16:32], invc[:, g * 16:(g + 1) * 16], ALU.mult,
                           reads=[fkey, ("invc",)], writes=[("tmp", t)])
                    dve_tt(arena[:, DM + c, 0:16], tmps[t][:, 0:16], pbuf[:, 16:32], ALU.subtract,
                           reads=[("tmp", t), ("pb", pi, 0)], writes=[("ar", DM + c, i)])

            gs = [acquire(("m12", l, c)) for c in range(GRP)]
            for i in range(NT):
                for c in range(GRP):
                    m12_block(gs[c], c, i)
                    if i == 0 and c == 1 and hook_i1 is not None:
                        hook_i1()
            release(GRP)
            for c in range(GRP, KC):
                s = acquire(("m12", l, c))
                for i in range(NT):
                    m12_block(s, c, i)
                release()
            s = acquire(("m3", l))
            sv = slots[s][:, 0:2048].rearrange("p (k c) -> p k c", k=8)
            for g in range(4):
                for jj in range(2):
                    c = 2 * g + jj
                    for i in range(NT):
                        bk = next_ps()
                        mm_group(bk, [(sv[:, 2 * g + kk, jj * 128:(jj + 1) * 128], arena[:, DM + 2 * g + kk, tsl(i)])
                                      for kk in range(2)],
                                 reads=[("w", s)] + [("ar", DM + 2 * g + kk, i) for kk in range(2)])
                        act(arena[:, PG + c, tsl(i)], psb[bk][:], AF.Identity,
                            reads=[("ps", bk), ("bps",), ("sm",)], writes=[("ar", PG + c, i)],
                            bias=bps[:, l * 8 + c:l * 8 + c + 1], scale=smc(l, SM_PS + c))
            release()
            for c in range(KC):
                s = acquire(("m4", l, c))
                sv = slots[s][:, 0:4096].rearrange("p (k n c) -> p k n c", k=KC, n=4)
                for i in range(NT):
                    res = []
                    for br, (wsel, gsel, src_off, bcol) in enumerate(((0, 2, UA, SM_BG + c), (1, 3, PG, SM_BG + 8 + c))):
                        by, bgt = next_ps(), next_ps()
                        mm_group(by, [(sv[:, k, wsel, :], arena[:, src_off + k, tsl(i)]) for k in range(KC)],
                                 reads=[("w", s)] + [("ar", src_off + k, i) for k in range(KC)])
                        mm_group(bgt, [(sv[:, k, gsel, :], hT[:, k, tsl(i)]) for k in range(KC)],
                                 reads=[("w", s)] + hreads(i))
                        tg, ty = next_tmp(), next_tmp()
                        act(tmps[tg][:], psb[bgt][:], AF.Sigmoid, reads=[("ps", bgt), ("sm",)], writes=[("tmp", tg)],
                            bias=smc(l, bcol))
                        dve_tt(tmps[ty][:], tmps[tg][:], psb[by][:], ALU.mult,
                               reads=[("tmp", tg), ("ps", by)], writes=[("tmp", ty)])
                        res.append(ty)
                    dve_tt(arena[:, DM + c, tsl(i)], tmps[res[0]][:], tmps[res[1]][:], ALU.add,
                           reads=[("tmp", res[0]), ("tmp", res[1])], writes=[("ar", DM + c, i)])
                release()
            pend = [None] * NT
            for i in range(NT):
                for ob in range(2):
                    s = acquire(("m5", l, ob, i))
                    sv = slots[s][:, 0:4096].rearrange("p (k c) -> p k c", k=KC)
                    for cc in range(4):
                        c = 4 * ob + cc
                        bo = next_ps()
                        mm_group(bo, [(sv[:, k, cc * 128:(cc + 1) * 128], arena[:, DM + k, tsl(i)]) for k in range(KC)],
                                 reads=[("w", s)] + [("ar", DM + k, i) for k in range(KC)])
                        post_evac(bo, c, i, pend)
                    release()
                    if i == NT - 1 and ob == 0 and next_pre is not None:
                        ss_mm(i, *pend[i])
                        pend[i] = None
                        next_pre(0)
                tail(l, 1, pend, i)

        P.emit("sp", (lambda e: e.dma_start(out=sm[:], in_=sm_d[:, :])), writes=[("sm",)], stream="cst")
        P.emit("sp", (lambda e: e.dma_start(out=ident[:], in_=id_d[:, :])), writes=[("ident",)], stream="cst2")
        P.emit("dve", (lambda e: e.memset(ones[:], 1.0)), writes=[("ones",)])
        for g, win in enumerate(POOL_WINDOWS):
            P.emit("dve", (lambda e, g=g, win=win: e.memset(invc[:, g * 16:(g + 1) * 16], 1.0 / win)), writes=[("invc",)])
            for t_ in range(win - 1):
                P.emit("dve", (lambda e, g=g, t_=t_: e.memset(invc[:, g * 16 + t_:g * 16 + t_ + 1], 1.0 / (t_ + 1))),
                       writes=[("invc",)])
        eps_t = sb("eps_t", [128, 1], F32)
        EPS_AP = [eps_t[:, 0:1]]
        P.emit("dve", (lambda e: e.memset(eps_t[:], EPS)), writes=[("eps",)])
        for l in range(depth):
            for gidx, (col, fac) in enumerate(((SM_F1POST, 0.5), (SM_MPOST, 1.0), (SM_F2POST, 0.5))):
                o = l * 24 + gidx * 8
                c0 = l * SM_PER_LAYER + col
                P.emit("dve", (lambda e, o=o, c0=c0, fac=fac: e.tensor_scalar(
                    out=gp05[:, o:o + 8], in0=sm[:, c0:c0 + 8], scalar1=fac, scalar2=None, op0=ALU.mult)),
                    reads=[("sm",)], writes=[("gp05",)])
            c0 = l * SM_PER_LAYER
            P.emit("dve", (lambda e, l=l, c0=c0: e.tensor_tensor(
                out=bps[:, l * 8:l * 8 + 8], in0=sm[:, c0 + SM_BPG:c0 + SM_BPG + 8],
                in1=sm[:, c0 + SM_PS:c0 + SM_PS + 8], op=ALU.mult)), reads=[("sm",)], writes=[("bps",)])

        prefetch()
        tiles_per_seq = seq // tt
        assert NT == 2

        def xload(ti, i):
            tok = ti * tt + i * 512
            for h in range(2):
                src = x_d[tok:tok + 512, h * 512:(h + 1) * 512].rearrange("(s p) t -> p s t", p=128)
                dst = houtv[:, :, tsl(i)].rearrange("p (s h) t -> p s h t", h=2)[:, :, h, :]
                P.emit("sp", (lambda e, dst=dst, src=src: e.dma_start(out=dst, in_=src)),
                       writes=[("hout", c, i) for c in range(KC)], stream=f"xin{i}")

        def in_transposes(i):
            for k in range(KC):
                h, t0 = k // 4, (k % 4) * 128
                bk = next_ps()

                def fn(e, k=k, h=h, t0=t0, bk=bk):
                    inst = None
                    for s4 in range(4):
                        inst = e.transpose(out=psb[bk][:, s4 * 128:(s4 + 1) * 128],
                                           in_=houtv[:, 2 * s4 + h, i * 512 + t0:i * 512 + t0 + 128],
                                           identity=ident[:])
                    return inst
                P.emit("pe", fn, reads=[("hout", c, i) for c in range(KC)] + [("ident",)], writes=[("ps", bk)])
                if k % 2 == 0:
                    act(xT[:, k, tsl(i)], psb[bk][:], AF.Copy, reads=[("ps", bk)], writes=[("xT", k, i)])
                else:
                    dve_copy(xT[:, k, tsl(i)], psb[bk][:], reads=[("ps", bk)], writes=[("xT", k, i)])

        def output_half(ti, i):
            for s4 in range(4):
                for h in range(2):
                    bk = next_ps()

                    def fn(e, s4=s4, h=h, bk=bk):
                        inst = None
                        for k4 in range(4):
                            k = h * 4 + k4
                            inst = e.transpose(out=psb[bk][:, k4 * 128:(k4 + 1) * 128],
                                               in_=xT[:, k, i * 512 + s4 * 128:i * 512 + (s4 + 1) * 128],
                                               identity=ident[:])
                        return inst
                    P.emit("pe", fn, reads=[("xT", h * 4 + k4, i) for k4 in range(4)] + [("ident",)],
                           writes=[("ps", bk)])
                    t = next_tmp()
                    if (s4 + h) % 2 == 0:
                        act(tmps[t][:], psb[bk][:], AF.Copy, reads=[("ps", bk)], writes=[("tmp", t)])
                    else:
                        dve_copy(tmps[t][:], psb[bk][:], reads=[("ps", bk)], writes=[("tmp", t)])
                    r0 = ti * tt + i * 512 + s4 * 128
                    P.emit("sp", (lambda e, t=t, r0=r0, h=h: e.dma_start(
                        out=out_d[r0:r0 + 128, h * 512:(h + 1) * 512], in_=tmps[t][:])),
                        reads=[("tmp", t)], stream=f"xo{t}")

        subs = []
        for l in range(depth):
            subs += [("ffn", l, 1), ("mix", l, 0), ("ffn", l, 2)]

        def pre_fn(sub):
            kind, l_, which = sub
            gcol = SM_MPRE if kind == "mix" else (SM_F1PRE if which == 1 else SM_F2PRE)
            return lambda i, l_=l_, gcol=gcol: pre_norm_tile(l_, gcol, i)

        for ti in range(n_tt):
            seq_start = (ti % tiles_per_seq == 0)
            if ti == 0:
                xload(0, 0)
                xload(0, 1)
                in_transposes(0)
                pre_fn(subs[0])(0)
            if seq_start:
                P.emit("dve", (lambda e: e.memset(st_cv[:], 0.0)),
                       writes=[("stcv", l, c) for l in range(depth) for c in range(KC)])
                P.emit("dve", (lambda e: e.memset(st_p[:], 0.0)),
                       writes=[("stp", l, c) for l in range(depth) for c in range(KC)])
            for si, sub in enumerate(subs):
                mypre = pre_fn(sub)
                first, last = (si == 0), (si == len(subs) - 1)

                def hook(mypre=mypre, first=first, ti=ti):
                    if first:
                        if ti > 0:
                            xload(ti, 1)
                            output_half(ti - 1, 1)
                        in_transposes(1)
                    mypre(1)

                if not last:
                    nxt = pre_fn(subs[si + 1])
                else:
                    def nxt(i, ti=ti):
                        assert i == 0
                        if ti + 1 < n_tt:
                            xload(ti + 1, 0)
                        output_half(ti, 0)
                        if ti + 1 < n_tt:
                            in_transposes(0)
                            pre_fn(subs[0])(0)
                if sub[0] == "ffn":
                    ffn(sub[1], sub[2], hook, nxt)
                else:
                    mixer(sub[1], seq_start, hook, nxt)
        output_half(n_tt - 1, 1)
        P.emit("sp", None, reads=[], writes=[("tmp", t) for t in range(ntmp)])
        assert wstate["cur"] == len(sched)

        with nc.Block() as block:
            @block.tensor
            def _(e):
                P.replay("pe", e, sems)

            @block.scalar
            def _(e):
                P.replay("act", e, sems)

            @block.vector
            def _(e):
                P.replay("dve", e, sems)

            @block.gpsimd
            def _(e):
                P.replay("pool", e, sems)

            @block.sync
            def _(e):
                P.replay("sp", e, sems)
    return nc


def pack_smalls(inp, depth):
    sm = np.zeros((128, SM_PER_LAYER * depth), np.float32)

    def put(l, col, vec):
        v = np.asarray(vec, np.float32).reshape(-1, 128)
        sm[:, l * SM_PER_LAYER + col:l * SM_PER_LAYER + col + v.shape[0]] = v.T

    for l in range(depth):
        put(l, SM_F1PRE, inp["ffn1_pre"][l])
        put(l, SM_F1POST, inp["ffn1_post"][l])
        put(l, SM_MPRE, inp["mix_pre"][l])
        put(l, SM_MPOST, inp["mix_post"][l])
        put(l, SM_F2PRE, inp["ffn2_pre"][l])
        put(l, SM_F2POST, inp["ffn2_post"][l])
        put(l, SM_BG, inp["b_gate"][l])
        put(l, SM_CW, inp["conv_w"][l])
        put(l, SM_BPG, inp["b_pool_group"][l])
        put(l, SM_PS, inp["pool_scale"][l])
    return sm


def make_in_maps(inp, depth, ncores, xs):
    common = {"smalls": pack_smalls(inp, depth), "ident": np.eye(128, dtype=np.float32)}
    for l in range(depth):
        def pack_gu(w):
            w = np.asarray(w, np.float32)
            blocks = []
            for jb in range(FC // 2):
                blocks += [w[:, jb * 256:(jb + 1) * 256], w[:, DFF + jb * 256:DFF + (jb + 1) * 256]]
            return np.ascontiguousarray(np.concatenate(blocks, axis=1))

        win = np.asarray(inp["w_in"][l], np.float32)
        wco = np.asarray(inp["w_conv_out"][l], np.float32)
        wpo = np.asarray(inp["w_pool_out"][l], np.float32)
        m12, m4 = [], []
        for c in range(KC):
            cs = slice(c * 128, (c + 1) * 128)
            m12 += [win[:, s_ * D + c * 128:s_ * D + (c + 1) * 128] for s_ in range(4)]
            m4 += [wco[:, cs], wpo[:, cs], win[:, 4 * D + c * 128:4 * D + (c + 1) * 128],
                   win[:, 5 * D + c * 128:5 * D + (c + 1) * 128]]
        common[f"f1gu{l}"] = pack_gu(inp["ffn1_w_gate_up"][l])
        common[f"f1d{l}"] = np.ascontiguousarray(inp["ffn1_w_down"][l], np.float32)
        common[f"m12w{l}"] = np.ascontiguousarray(np.concatenate(m12, axis=1))
        common[f"m4w{l}"] = np.ascontiguousarray(np.concatenate(m4, axis=1))
        common[f"wpg{l}"] = np.ascontiguousarray(inp["w_pool_group"][l], np.float32).reshape(4 * 256, 256)
        common[f"wo{l}"] = np.ascontiguousarray(inp["w_o"][l], np.float32)
        common[f"f2gu{l}"] = pack_gu(inp["ffn2_w_gate_up"][l])
        common[f"f2d{l}"] = np.ascontiguousarray(inp["ffn2_w_down"][l], np.float32)
    maps = []
    for c in range(ncores):
        m = dict(common)
        m["x"] = xs[c]
        maps.append(m)
    return maps


_NC_CACHE = {}


def kernel(**inputs):
    inp = {k: np.asarray(v) for k, v in inputs.items()}
    x = inp["x"].astype(np.float32, copy=False)
    B, S, _ = x.shape
    depth = inp["ffn1_pre"].shape[0]
    nseq = B // NCORES
    xs = [np.ascontiguousarray(x[c * nseq:(c + 1) * nseq].reshape(nseq * S, D)) for c in range(NCORES)]
    key = (nseq, S, depth)
    if key not in _NC_CACHE:
        _NC_CACHE[key] = build_program(nseq, S, depth)
    nc = _NC_CACHE[key]
    res = run_bass_kernel_spmd(nc, make_in_maps(inp, depth, NCORES, xs), core_ids=list(range(NCORES)))
    out = np.stack([r["out"].reshape(nseq, S, D) for r in res.results], axis=0).reshape(B, S, D)
    return out.astype(np.float32, copy=False)
```

```python
import bisect
from contextlib import ExitStack

import numpy as np
import concourse.bass as bass
import concourse.mybir as mybir
from concourse.bass_utils import run_bass_kernel_spmd

F32 = mybir.dt.float32
BF16 = mybir.dt.bfloat16
AF = mybir.ActivationFunctionType
ALU = mybir.AluOpType

D = 1024
KC = 8
DFF = 2816
FC = 22
EPS = 1e-6
NCORES = 8
POOL_WINDOWS = (2, 4, 8, 16)
RSTD_MODE = "lnexp"

ENGS = ("pe", "act", "dve", "pool", "sp")

SM_F1PRE, SM_F1POST, SM_MPRE, SM_MPOST, SM_F2PRE, SM_F2POST = 0, 8, 16, 24, 32, 40
SM_BG = 48
SM_CW = 64
SM_BPG = 88
SM_PS = 96
SM_PER_LAYER = 104


class Prog:
    def __init__(self):
        self.ins = {e: [] for e in ENGS}
        self.res = {}
        self.seen = {e: {} for e in ENGS}
        self.scount = {}
        self.milestones = {e: set() for e in ENGS}

    def emit(self, eng, fn, reads=(), writes=(), stream=None):
        idx = len(self.ins[eng])
        if stream is None:
            me = (eng, idx)
        else:
            sidx = self.scount.get(stream, 0)
            self.scount[stream] = sidx + 1
            me = (stream, sidx)
        deps = {}

        def add(dep, kind):
            if dep is None:
                return
            src, i = dep
            if stream is None and src == eng:
                if eng == "pe":
                    return
            if i > deps.get(src, -1):
                deps[src] = i

        for r in reads:
            st = self.res.get(r)
            if st is not None:
                add(st["w"], "raw")
        for w in writes:
            st = self.res.get(w)
            if st is not None:
                add(st["w"], "waw")
                for rd in st["r"]:
                    add(rd, "war")
        waits = []
        for src, i in deps.items():
            if i > self.seen[eng].get(src, -1):
                self.seen[eng][src] = i
                waits.append((src, i))
                if src in self.milestones:
                    self.milestones[src].add(i)
        for r in reads:
            st = self.res.setdefault(r, {"w": None, "r": []})
            st["r"].append(me)
        for w in writes:
            self.res[w] = {"w": me, "r": []}
        self.ins[eng].append((waits, fn, stream, idx))
        return me

    def replay(self, eng, e, sems):
        ms = sorted(self.milestones[eng])
        msorted = {s: sorted(self.milestones[s]) for s in self.milestones}

        def semval(src, i):
            if src in msorted:
                return bisect.bisect_right(msorted[src], i)
            return 16 * (i + 1)

        msset = set(ms)
        for waits, fn, stream, idx in self.ins[eng]:
            for src, i in waits:
                e.wait_ge(sems[src], semval(src, i))
            if fn is None:
                continue
            inst = fn(e)
            if stream is not None:
                inst.then_inc(sems[stream], 16)
            elif idx in msset:
                inst.then_inc(sems[eng], 1)


def build_program(nseq, seq, depth, tt=1024, nslots=4, ntmp=11):
    NT = tt // 512
    assert tt % 512 == 0 and seq % tt == 0
    ntok = nseq * seq
    nsub = tt // 128
    nc = bass.Bass("TRN2", target_bir_lowering=False)

    x_d = nc.dram_tensor("x", [ntok, D], F32, kind="ExternalInput").ap()
    out_d = nc.dram_tensor("out", [ntok, D], F32, kind="ExternalOutput").ap()
    sm_d = nc.dram_tensor("smalls", [128, SM_PER_LAYER * depth], F32, kind="ExternalInput").ap()
    id_d = nc.dram_tensor("ident", [128, 128], F32, kind="ExternalInput").ap()
    W = []
    for l in range(depth):
        d = {}
        d["f1gu"] = nc.dram_tensor(f"f1gu{l}", [D, 2 * DFF], F32, kind="ExternalInput").ap()
        d["f1d"] = nc.dram_tensor(f"f1d{l}", [DFF, D], F32, kind="ExternalInput").ap()
        d["m12w"] = nc.dram_tensor(f"m12w{l}", [D, 4 * D], F32, kind="ExternalInput").ap()
        d["m4w"] = nc.dram_tensor(f"m4w{l}", [D, 4 * D], F32, kind="ExternalInput").ap()
        d["wpg"] = nc.dram_tensor(f"wpg{l}", [4 * 256, 256], F32, kind="ExternalInput").ap()
        d["wo"] = nc.dram_tensor(f"wo{l}", [D, D], F32, kind="ExternalInput").ap()
        d["f2gu"] = nc.dram_tensor(f"f2gu{l}", [D, 2 * DFF], F32, kind="ExternalInput").ap()
        d["f2d"] = nc.dram_tensor(f"f2d{l}", [DFF, D], F32, kind="ExternalInput").ap()
        W.append(d)

    def kview(ap2d, c0, c1):
        return ap2d.rearrange("(k p) c -> p k c", p=128)[:, :, c0:c1]

    P = Prog()
    es = ExitStack()
    with es:
        def sb(name, shape, dt):
            return es.enter_context(nc.sbuf_tensor("sb_" + name, shape, dt))

        xT = sb("xT", [128, KC, tt], F32)
        hT = sb("hT", [128, KC, tt], BF16)
        arena = sb("arena", [128, 24, tt], BF16)
        hout = sb("hout", [128, KC * tt], F32)
        houtv = hout[:].rearrange("p (c t) -> p c t", t=tt)
        stagev = hout[:].rearrange("p (s d) -> p s d", d=D)
        tmps = [sb(f"tmp{i}", [128, 512], F32) for i in range(ntmp)]
        sqs = [sb(f"sq{i}", [128, 512], BF16) for i in range(4)]
        cvb = [sb(f"cv{i}", [128, 2 + 512], F32) for i in range(2)]
        pbs = [[sb(f"pb{i}_{j}", [128, 16 + 512], F32) for j in range(3)] for i in range(2)]
        slots = [sb(f"ws{i}", [128, 4096], BF16) for i in range(nslots)]
        ones = sb("ones", [128, 128], BF16)
        ident = sb("ident", [128, 128], F32)
        sm = sb("sm", [128, SM_PER_LAYER * depth], F32)
        gp05 = sb("gp05", [128, depth * 24], F32)
        bps = sb("bps", [128, depth * 8], F32)
        invc = sb("invc", [128, 4 * 16], F32)
        st_cv = sb("st_cv", [128, depth * KC * 2], F32)
        st_p = sb("st_p", [128, depth * KC * 16], F32)
        psb = [es.enter_context(nc.psum_tensor(f"ps{i}", [128, 512], F32)) for i in range(8)]

        sem_names = list(ENGS) + [f"w{i}" for i in range(nslots)] + ["xin0", "xin1", "cst", "cst2"] + [f"xo{i}" for i in range(ntmp)]
        sems = {n: es.enter_context(nc.semaphore(f"s_{n}")) for n in sem_names}

        rot = {"ps": 0, "tmp": 0, "sq": 0, "cv": 0, "pb": 0}

        def next_ps():
            i = rot["ps"]
            rot["ps"] = (i + 1) % 6
            return i

        def next_tmp():
            i = rot["tmp"]
            rot["tmp"] = (i + 1) % ntmp
            return i

        def next_sq():
            i = rot["sq"]
            rot["sq"] = (i + 1) % 4
            return i

        def tsl(i):
            return slice(i * 512, (i + 1) * 512)

        sched = []

        def flat(c0, c1):
            return lambda s: s[:, c0:c1].rearrange("p (k c) -> p k c", k=KC)

        def sub3(n, j):
            return lambda s: s[:, 0:KC * n * 128].rearrange("p (k n c) -> p k n c", k=KC, n=n)[:, :, j, :]

        def ffn_blocks(l, which):
            gu = W[l]["f1gu" if which == 1 else "f2gu"]
            dn = W[l]["f1d" if which == 1 else "f2d"]
            for jb in range(FC // 2):
                sched.append((("gu", l, which, jb), [
                    (flat(0, 4096), kview(gu, jb * 512, jb * 512 + 512))]))
            for i in range(NT):
                for c in range(KC):
                    sched.append((("dn", l, which, c, i), [
                        (lambda s: s[:, 0:FC * 128].rearrange("p (k c) -> p k c", k=FC),
                         kview(dn, c * 128, c * 128 + 128)),
                    ]))

        def mixer_blocks(l):
            for c in range(KC):
                sched.append((("m12", l, c), [
                    (flat(0, 4096), kview(W[l]["m12w"], c * 512, c * 512 + 512))]))
            sched.append((("m3", l), [
                (lambda s: s[:, 0:2048].rearrange("p (k c) -> p k c", k=8), kview(W[l]["wpg"], 0, 256))]))
            for c in range(KC):
                sched.append((("m4", l, c), [
                    (flat(0, 4096), kview(W[l]["m4w"], c * 512, c * 512 + 512))]))
            for i in range(NT):
                for ob in range(2):
                    sched.append((("m5", l, ob, i), [
                        (flat(0, 4096), kview(W[l]["wo"], ob * 512, ob * 512 + 512))]))

        n_tt = ntok // tt
        for _t in range(n_tt):
            for l in range(depth):
                ffn_blocks(l, 1)
                mixer_blocks(l)
                ffn_blocks(l, 2)

        wstate = {"cur": 0, "emitted": 0, "released": 0}

        def emit_wdma(m):
            tag, dmas = sched[m]
            s = m % nslots
            for dst_fn, src in dmas:
                dst = dst_fn(slots[s])
                P.emit("pool", (lambda e, dst=dst, src=src: e.dma_start(out=dst, in_=src)),
                       writes=[("w", s)], stream=f"w{s}")

        def prefetch():
            upto = min(wstate["released"] + nslots, len(sched))
            for m in range(wstate["emitted"], upto):
                emit_wdma(m)
            wstate["emitted"] = max(wstate["emitted"], upto)

        def acquire(tag):
            n = wstate["cur"]
            assert sched[n][0] == tag, (sched[n][0], tag)
            assert n < wstate["emitted"], "weight block not prefetched"
            wstate["cur"] = n + 1
            return n % nslots

        def release(count=1):
            wstate["released"] += count
            assert wstate["released"] <= wstate["cur"]
            prefetch()

        def mm_group(bank, pairs, reads):
            n = len(pairs)

            def fn(e, pairs=pairs, bank=bank, n=n):
                inst = None
                for q, (l_, r_) in enumerate(pairs):
                    inst = e.matmul(out=psb[bank][:], lhsT=l_, rhs=r_, start=(q == 0), stop=(q == n - 1))
                return inst
            P.emit("pe", fn, reads=reads, writes=[("ps", bank)])

        def act(out, in_, func, reads, writes, bias=None, scale=None):
            kw = {}
            if bias is not None:
                kw["bias"] = bias
            if scale is not None:
                kw["scale"] = scale
            P.emit("act", (lambda e: e.activation(out=out, in_=in_, func=func, **kw)), reads=reads, writes=writes)

        def dve_tt(out, in0, in1, op, reads, writes):
            P.emit("dve", (lambda e: e.tensor_tensor(out=out, in0=in0, in1=in1, op=op)), reads=reads, writes=writes)

        def dve_stt(out, in0, scalar, in1, op0, op1, reads, writes):
            P.emit("dve", (lambda e: e.scalar_tensor_tensor(out=out, in0=in0, scalar=scalar, in1=in1,
                                                            op0=op0, op1=op1)), reads=reads, writes=writes)

        def dve_copy(out, in_, reads, writes):
            P.emit("dve", (lambda e: e.tensor_copy(out=out, in_=in_)), reads=reads, writes=writes)

        def smc(l, col):
            c = l * SM_PER_LAYER + col
            return sm[:, c:c + 1]

        NPS = [6, 7]

        def emit_rstd(i):
            b = NPS[i]
            if RSTD_MODE == "lnexp":
                t = next_tmp()
                act(tmps[t][:], psb[b][:], AF.Ln, reads=[("ps", b), ("eps",)], writes=[("tmp", t)],
                    bias=EPS_AP[0], scale=1.0 / D)
                act(psb[b][:], tmps[t][:], AF.Exp, reads=[("tmp", t)], writes=[("ps", b)], scale=-0.5)
            else:
                t = next_tmp()
                act(tmps[t][:], psb[b][:], AF.Sqrt, reads=[("ps", b), ("eps",)], writes=[("tmp", t)],
                    bias=EPS_AP[0], scale=1.0 / D)
                P.emit("dve", (lambda e: e.reciprocal(out=psb[b][:], in_=tmps[t][:])),
                       reads=[("tmp", t)], writes=[("ps", b)])

        def ss_mm(i, q, first, last):
            b = NPS[i]

            def fn(e):
                return e.matmul(out=psb[b][:], lhsT=ones[:], rhs=sqs[q][:], start=first, stop=last)
            P.emit("pe", fn, reads=[("sq", q), ("ones",)], writes=[("ps", b)])

        def pre_norm_tile(l, gcol, i):
            pend = None
            for k in range(KC):
                q = next_sq()
                act(sqs[q][:], xT[:, k, tsl(i)], AF.Square, reads=[("xT", k, i)], writes=[("sq", q)])
                if pend is not None:
                    ss_mm(i, *pend)
                pend = (q, k == 0, k == KC - 1)
            ss_mm(i, *pend)
            emit_rstd(i)
            b = NPS[i]
            for k in range(KC):
                dve_stt(hT[:, k, tsl(i)], xT[:, k, tsl(i)], smc(l, gcol + k), psb[b][:], ALU.mult, ALU.mult,
                        reads=[("xT", k, i), ("ps", b), ("sm",)], writes=[("hT", k, i)])

        def post_evac(bank, c, i, pend):
            act(houtv[:, c, tsl(i)], psb[bank][:], AF.Copy, reads=[("ps", bank)], writes=[("hout", c, i)])
            q = next_sq()
            act(sqs[q][:], psb[bank][:], AF.Square, reads=[("ps", bank)], writes=[("sq", q)])
            if pend[i] is not None:
                ss_mm(i, *pend[i])
            pend[i] = (q, c == 0, c == KC - 1)

        def tail(l, gidx, pend, i):
            ss_mm(i, *pend[i])
            pend[i] = None
            emit_rstd(i)
            b = NPS[i]
            for c in range(KC):
                t = next_tmp()
                gc = l * 24 + gidx * 8 + c
                dve_stt(tmps[t][:], houtv[:, c, tsl(i)], gp05[:, gc:gc + 1], psb[b][:], ALU.mult, ALU.mult,
                        reads=[("hout", c, i), ("ps", b), ("gp05",)], writes=[("tmp", t)])
                dve_tt(xT[:, c, tsl(i)], xT[:, c, tsl(i)], tmps[t][:], ALU.add,
                       reads=[("xT", c, i), ("tmp", t)], writes=[("xT", c, i)])

        GRP = 3

        def hreads(i):
            return [("hT", k, i) for k in range(KC)]

        def ffn(l, which, hook_i1, next_pre):
            def up_block(s, jb, i):
                sv = slots[s][:, 0:4096].rearrange("p (k n c) -> p k n c", k=KC, n=2)
                for jj in range(2):
                    j = 2 * jb + jj
                    bg, bu = next_ps(), next_ps()
                    mm_group(bg, [(sv[:, k, 0, jj * 128:(jj + 1) * 128], hT[:, k, tsl(i)]) for k in range(KC)],
                             reads=[("w", s)] + hreads(i))
                    mm_group(bu, [(sv[:, k, 1, jj * 128:(jj + 1) * 128], hT[:, k, tsl(i)]) for k in range(KC)],
                             reads=[("w", s)] + hreads(i))
                    t = next_tmp()
                    act(tmps[t][:], psb[bg][:], AF.Silu, reads=[("ps", bg)], writes=[("tmp", t)])
                    dve_tt(arena[:, j, tsl(i)], tmps[t][:], psb[bu][:], ALU.mult,
                           reads=[("tmp", t), ("ps", bu)], writes=[("ar", j, i)])

            gs = [acquire(("gu", l, which, jb)) for jb in range(GRP)]
            for i in range(NT):
                for jb in range(GRP):
                    up_block(gs[jb], jb, i)
                    if i == 0 and jb == 1 and hook_i1 is not None:
                        hook_i1()
            release(GRP)
            for jb in range(GRP, FC // 2):
                s = acquire(("gu", l, which, jb))
                for i in range(NT):
                    up_block(s, jb, i)
                release()
            pend = [None] * NT
            gidx = 0 if which == 1 else 2
            deferred = None
            for i in range(NT):
                for c in range(KC):
                    s = acquire(("dn", l, which, c, i))
                    sv = slots[s][:, 0:FC * 128].rearrange("p (k c) -> p k c", k=FC)
                    bo = next_ps()
                    mm_group(bo, [(sv[:, j, :], arena[:, j, tsl(i)]) for j in range(FC)],
                             reads=[("w", s)] + [("ar", j, i) for j in range(FC)])
                    if deferred is not None:
                        tail(l, gidx, pend, deferred)
                        deferred = None
                    post_evac(bo, c, i, pend)
                    release()
                    if i == NT - 1 and c == 3 and next_pre is not None:
                        ss_mm(i, *pend[i])
                        pend[i] = None
                        next_pre(0)
                if i < NT - 1:
                    deferred = i
                else:
                    tail(l, gidx, pend, i)

        def mixer(l, seq_start, hook_i1, next_pre):
            UA, DM, PG = 0, 8, 16

            def act_copy(out, in_, reads, writes):
                act(out, in_, AF.Copy, reads=reads, writes=writes)

            def m12_block(s, c, i):
                sv = slots[s][:, 0:4096].rearrange("p (k n c) -> p k n c", k=KC, n=4)
                stc = st_cv[:, (l * KC + c) * 2:(l * KC + c) * 2 + 2]
                g = c // 2
                win = POOL_WINDOWS[g]
                stp = st_p[:, (l * KC + c) * 16:(l * KC + c) * 16 + 16]
                bv, bc, bp, bb = next_ps(), next_ps(), next_ps(), next_ps()
                for n_, bk in ((2, bv), (1, bc), (3, bp), (0, bb)):
                    mm_group(bk, [(sv[:, k, n_, :], hT[:, k, tsl(i)]) for k in range(KC)],
                             reads=[("w", s)] + hreads(i))
                tv = next_tmp()
                act(tmps[tv][:], psb[bv][:], AF.Copy, reads=[("ps", bv)], writes=[("tmp", tv)])
                tb = next_tmp()
                act(tmps[tb][:], psb[bb][:], AF.Copy, reads=[("ps", bb)], writes=[("tmp", tb)])
                cv = rot["cv"]
                rot["cv"] = 1 - cv
                cb = cvb[cv]
                act_copy(cb[:, 0:2], stc, reads=[("stcv", l, c)], writes=[("cv", cv)])
                dve_tt(cb[:, 2:514], tmps[tv][:], psb[bc][:], ALU.mult,
                       reads=[("tmp", tv), ("ps", bc)], writes=[("cv", cv)])
                act_copy(stc, cb[:, 512:514], reads=[("cv", cv)], writes=[("stcv", l, c)])
                t0, t1, t2 = next_tmp(), next_tmp(), next_tmp()
                act(tmps[t0][:], cb[:, 0:512], AF.Copy, reads=[("cv", cv), ("sm",)], writes=[("tmp", t0)],
                    scale=smc(l, SM_CW + 0 * 8 + c))
                dve_stt(tmps[t1][:], cb[:, 1:513], smc(l, SM_CW + 1 * 8 + c), tmps[t0][:], ALU.mult, ALU.add,
                        reads=[("cv", cv), ("tmp", t0), ("sm",)], writes=[("tmp", t1)])
                dve_stt(tmps[t2][:], cb[:, 2:514], smc(l, SM_CW + 2 * 8 + c), tmps[t1][:], ALU.mult, ALU.add,
                        reads=[("cv", cv), ("tmp", t1), ("sm",)], writes=[("tmp", t2)])
                dve_tt(arena[:, UA + c, tsl(i)], tmps[t2][:], tmps[tb][:], ALU.mult,
                       reads=[("tmp", t2), ("tmp", tb)], writes=[("ar", UA + c, i)])
                pi = rot["pb"]
                rot["pb"] = 1 - pi
                pbuf, sA, sB = pbs[pi]
                act_copy(pbuf[:, 0:16], stp, reads=[("stp", l, c)], writes=[("pb", pi, 0)])
                act(pbuf[:, 16:528], psb[bp][:], AF.Copy, reads=[("ps", bp)], writes=[("pb", pi, 0)])
                act_copy(stp, pbuf[:, 512:528], reads=[("pb", pi, 0)], writes=[("stp", l, c)])
                src, dst, skey, dkey = pbuf, sA, ("pb", pi, 0), ("pb", pi, 1)
                step, lo = 1, 1
                while step < win:
                    dve_tt(dst[:, lo:528], src[:, lo:528], src[:, lo - step:528 - step], ALU.add,
                           reads=[skey], writes=[dkey])
                    step *= 2
                    lo = 2 * step - 1
                    if src is pbuf:
                        src, skey = sA, ("pb", pi, 1)
                        dst, dkey = sB, ("pb", pi, 2)
                    else:
                        src, dst = dst, src
                        skey, dkey = dkey, skey
                fin, fkey = src, skey
                dve_stt(arena[:, DM + c, tsl(i)], fin[:, 16:528], 1.0 / win, pbuf[:, 16:528],
                        ALU.mult, ALU.subtract, reads=[fkey, ("pb", pi, 0)], writes=[("ar", DM + c, i)])
                if seq_start and i == 0:
                    t = next_tmp()
                    dve_tt(tmps[t][:, 0:16], fin[:, 16:32], invc[:, g * 16:(g + 1) * 16], ALU.mult,
                           reads=[fkey, ("invc",)], writes=[("tmp", t)])
                    dve_tt(arena[:, DM + c, 0:16], tmps[t][:, 0:16], pbuf[:, 16:32], ALU.subtract,
                           reads=[("tmp", t), ("pb", pi, 0)], writes=[("ar", DM + c, i)])

            gs = [acquire(("m12", l, c)) for c in range(GRP)]
            for i in range(NT):
                for c in range(GRP):
                    m12_block(gs[c], c, i)
                    if i == 0 and c == 0 and hook_i1 is not None:
                        hook_i1()
            release(GRP)
            for c in range(GRP, KC):
                s = acquire(("m12", l, c))
                for i in range(NT):
                    m12_block(s, c, i)
                release()
            s = acquire(("m3", l))
            sv = slots[s][:, 0:2048].rearrange("p (k c) -> p k c", k=8)
            for g in range(4):
                for jj in range(2):
                    c = 2 * g + jj
                    for i in range(NT):
                        bk = next_ps()
                        mm_group(bk, [(sv[:, 2 * g + kk, jj * 128:(jj + 1) * 128], arena[:, DM + 2 * g + kk, tsl(i)])
                                      for kk in range(2)],
                                 reads=[("w", s)] + [("ar", DM + 2 * g + kk, i) for kk in range(2)])
                        act(arena[:, PG + c, tsl(i)], psb[bk][:], AF.Identity,
                            reads=[("ps", bk), ("bps",), ("sm",)], writes=[("ar", PG + c, i)],
                            bias=bps[:, l * 8 + c:l * 8 + c + 1], scale=smc(l, SM_PS + c))
            release()
            for c in range(KC):
                s = acquire(("m4", l, c))
                sv = slots[s][:, 0:4096].rearrange("p (k n c) -> p k n c", k=KC, n=4)
                for i in range(NT):
                    res = []
                    for br, (wsel, gsel, src_off, bcol) in enumerate(((0, 2, UA, SM_BG + c), (1, 3, PG, SM_BG + 8 + c))):
                        by, bgt = next_ps(), next_ps()
                        mm_group(by, [(sv[:, k, wsel, :], arena[:, src_off + k, tsl(i)]) for k in range(KC)],
                                 reads=[("w", s)] + [("ar", src_off + k, i) for k in range(KC)])
                        mm_group(bgt, [(sv[:, k, gsel, :], hT[:, k, tsl(i)]) for k in range(KC)],
                                 reads=[("w", s)] + hreads(i))
                        tg, ty = next_tmp(), next_tmp()
                        act(tmps[tg][:], psb[bgt][:], AF.Sigmoid, reads=[("ps", bgt), ("sm",)], writes=[("tmp", tg)],
                            bias=smc(l, bcol))
                        dve_tt(tmps[ty][:], tmps[tg][:], psb[by][:], ALU.mult,
                               reads=[("tmp", tg), ("ps", by)], writes=[("tmp", ty)])
                        res.append(ty)
                    dve_tt(arena[:, DM + c, tsl(i)], tmps[res[0]][:], tmps[res[1]][:], ALU.add,
                           reads=[("tmp", res[0]), ("tmp", res[1])], writes=[("ar", DM + c, i)])
                release()
            pend = [None] * NT
            deferred = None
            for i in range(NT):
                for ob in range(2):
                    s = acquire(("m5", l, ob, i))
                    sv = slots[s][:, 0:4096].rearrange("p (k c) -> p k c", k=KC)
                    for cc in range(4):
                        c = 4 * ob + cc
                        bo = next_ps()
                        mm_group(bo, [(sv[:, k, cc * 128:(cc + 1) * 128], arena[:, DM + k, tsl(i)]) for k in range(KC)],
                                 reads=[("w", s)] + [("ar", DM + k, i) for k in range(KC)])
                        if deferred is not None:
                            tail(l, 1, pend, deferred)
                            deferred = None
                        post_evac(bo, c, i, pend)
                    release()
                    if i == NT - 1 and ob == 0 and next_pre is not None:
                        ss_mm(i, *pend[i])
                        pend[i] = None
                        next_pre(0)
                if i < NT - 1:
                    deferred = i
                else:
                    tail(l, 1, pend, i)

        P.emit("sp", (lambda e: e.dma_start(out=sm[:], in_=sm_d[:, :])), writes=[("sm",)], stream="cst")
        P.emit("sp", (lambda e: e.dma_start(out=ident[:], in_=id_d[:, :])), writes=[("ident",)], stream="cst2")
        P.emit("dve", (lambda e: e.memset(ones[:], 1.0)), writes=[("ones",)])
        for g, win in enumerate(POOL_WINDOWS):
            P.emit("dve", (lambda e, g=g, win=win: e.memset(invc[:, g * 16:(g + 1) * 16], 1.0 / win)), writes=[("invc",)])
            for t_ in range(win - 1):
                P.emit("dve", (lambda e, g=g, t_=t_: e.memset(invc[:, g * 16 + t_:g * 16 + t_ + 1], 1.0 / (t_ + 1))),
                       writes=[("invc",)])
        eps_t = sb("eps_t", [128, 1], F32)
        EPS_AP = [eps_t[:, 0:1]]
        P.emit("dve", (lambda e: e.memset(eps_t[:], EPS)), writes=[("eps",)])
        for l in range(depth):
            for gidx, (col, fac) in enumerate(((SM_F1POST, 0.5), (SM_MPOST, 1.0), (SM_F2POST, 0.5))):
                o = l * 24 + gidx * 8
                c0 = l * SM_PER_LAYER + col
                P.emit("dve", (lambda e, o=o, c0=c0, fac=fac: e.tensor_scalar(
                    out=gp05[:, o:o + 8], in0=sm[:, c0:c0 + 8], scalar1=fac, scalar2=None, op0=ALU.mult)),
                    reads=[("sm",)], writes=[("gp05",)])
            c0 = l * SM_PER_LAYER
            P.emit("dve", (lambda e, l=l, c0=c0: e.tensor_tensor(
                out=bps[:, l * 8:l * 8 + 8], in0=sm[:, c0 + SM_BPG:c0 + SM_BPG + 8],
                in1=sm[:, c0 + SM_PS:c0 + SM_PS + 8], op=ALU.mult)), reads=[("sm",)], writes=[("bps",)])

        prefetch()
        tiles_per_seq = seq // tt
        assert NT == 2

        def xload(ti, i):
            tok = ti * tt + i * 512
            for h in range(2):
                src = x_d[tok:tok + 512, h * 512:(h + 1) * 512].rearrange("(s p) t -> p s t", p=128)
                dst = houtv[:, :, tsl(i)].rearrange("p (s h) t -> p s h t", h=2)[:, :, h, :]
                P.emit("sp", (lambda e, dst=dst, src=src: e.dma_start(out=dst, in_=src)),
                       writes=[("hout", c, i) for c in range(KC)], stream=f"xin{i}")

        def in_transposes(i):
            for k in range(KC):
                h, t0 = k // 4, (k % 4) * 128
                bk = next_ps()

                def fn(e, k=k, h=h, t0=t0, bk=bk):
                    inst = None
                    for s4 in range(4):
                        inst = e.transpose(out=psb[bk][:, s4 * 128:(s4 + 1) * 128],
                                           in_=houtv[:, 2 * s4 + h, i * 512 + t0:i * 512 + t0 + 128],
                                           identity=ident[:])
                    return inst
                P.emit("pe", fn, reads=[("hout", c, i) for c in range(KC)] + [("ident",)], writes=[("ps", bk)])
                if k % 2 == 0:
                    act(xT[:, k, tsl(i)], psb[bk][:], AF.Copy, reads=[("ps", bk)], writes=[("xT", k, i)])
                else:
                    dve_copy(xT[:, k, tsl(i)], psb[bk][:], reads=[("ps", bk)], writes=[("xT", k, i)])

        def output_half(ti, i):
            for s4 in range(4):
                for h in range(2):
                    bk = next_ps()

                    def fn(e, s4=s4, h=h, bk=bk):
                        inst = None
                        for k4 in range(4):
                            k = h * 4 + k4
                            inst = e.transpose(out=psb[bk][:, k4 * 128:(k4 + 1) * 128],
                                               in_=xT[:, k, i * 512 + s4 * 128:i * 512 + (s4 + 1) * 128],
                                               identity=ident[:])
                        return inst
                    P.emit("pe", fn, reads=[("xT", h * 4 + k4, i) for k4 in range(4)] + [("ident",)],
                           writes=[("ps", bk)])
                    t = next_tmp()
                    if (s4 + h) % 2 == 0:
                        act(tmps[t][:], psb[bk][:], AF.Copy, reads=[("ps", bk)], writes=[("tmp", t)])
                    else:
                        dve_copy(tmps[t][:], psb[bk][:], reads=[("ps", bk)], writes=[("tmp", t)])
                    r0 = ti * tt + i * 512 + s4 * 128
                    P.emit("sp", (lambda e, t=t, r0=r0, h=h: e.dma_start(
                        out=out_d[r0:r0 + 128, h * 512:(h + 1) * 512], in_=tmps[t][:])),
                        reads=[("tmp", t)], stream=f"xo{t}")

        subs = []
        for l in range(depth):
            subs += [("ffn", l, 1), ("mix", l, 0), ("ffn", l, 2)]

        def pre_fn(sub):
            kind, l_, which = sub
            gcol = SM_MPRE if kind == "mix" else (SM_F1PRE if which == 1 else SM_F2PRE)
            return lambda i, l_=l_, gcol=gcol: pre_norm_tile(l_, gcol, i)

        for ti in range(n_tt):
            seq_start = (ti % tiles_per_seq == 0)
            if ti == 0:
                xload(0, 0)
                xload(0, 1)
                in_transposes(0)
                pre_fn(subs[0])(0)
            if seq_start:
                P.emit("dve", (lambda e: e.memset(st_cv[:], 0.0)),
                       writes=[("stcv", l, c) for l in range(depth) for c in range(KC)])
                P.emit("dve", (lambda e: e.memset(st_p[:], 0.0)),
                       writes=[("stp", l, c) for l in range(depth) for c in range(KC)])
            for si, sub in enumerate(subs):
                mypre = pre_fn(sub)
                first, last = (si == 0), (si == len(subs) - 1)

                def hook(mypre=mypre, first=first, ti=ti):
                    if first:
                        if ti > 0:
                            xload(ti, 1)
                            output_half(ti - 1, 1)
                        in_transposes(1)
                    mypre(1)

                if not last:
                    nxt = pre_fn(subs[si + 1])
                else:
                    def nxt(i, ti=ti):
                        assert i == 0
                        if ti + 1 < n_tt:
                            xload(ti + 1, 0)
                        output_half(ti, 0)
                        if ti + 1 < n_tt:
                            in_transposes(0)
                            pre_fn(subs[0])(0)
                if sub[0] == "ffn":
                    ffn(sub[1], sub[2], hook, nxt)
                else:
                    mixer(sub[1], seq_start, hook, nxt)
        output_half(n_tt - 1, 1)
        P.emit("sp", None, reads=[], writes=[("tmp", t) for t in range(ntmp)])
        assert wstate["cur"] == len(sched)

        with nc.Block() as block:
            @block.tensor
            def _(e):
                P.replay("pe", e, sems)

            @block.scalar
            def _(e):
                P.replay("act", e, sems)

            @block.vector
            def _(e):
                P.replay("dve", e, sems)

            @block.gpsimd
            def _(e):
                P.replay("pool", e, sems)

            @block.sync
            def _(e):
                P.replay("sp", e, sems)
    return nc


def pack_smalls(inp, depth):
    sm = np.zeros((128, SM_PER_LAYER * depth), np.float32)

    def put(l, col, vec):
        v = np.asarray(vec, np.float32).reshape(-1, 128)
        sm[:, l * SM_PER_LAYER + col:l * SM_PER_LAYER + col + v.shape[0]] = v.T

    for l in range(depth):
        put(l, SM_F1PRE, inp["ffn1_pre"][l])
        put(l, SM_F1POST, inp["ffn1_post"][l])
        put(l, SM_MPRE, inp["mix_pre"][l])
        put(l, SM_MPOST, inp["mix_post"][l])
        put(l, SM_F2PRE, inp["ffn2_pre"][l])
        put(l, SM_F2POST, inp["ffn2_post"][l])
        put(l, SM_BG, inp["b_gate"][l])
        put(l, SM_CW, inp["conv_w"][l])
        put(l, SM_BPG, inp["b_pool_group"][l])
        put(l, SM_PS, inp["pool_scale"][l])
    return sm


def make_in_maps(inp, depth, ncores, xs):
    common = {"smalls": pack_smalls(inp, depth), "ident": np.eye(128, dtype=np.float32)}
    for l in range(depth):
        def pack_gu(w):
            w = np.asarray(w, np.float32)
            blocks = []
            for jb in range(FC // 2):
                blocks += [w[:, jb * 256:(jb + 1) * 256], w[:, DFF + jb * 256:DFF + (jb + 1) * 256]]
            return np.ascontiguousarray(np.concatenate(blocks, axis=1))

        win = np.asarray(inp["w_in"][l], np.float32)
        wco = np.asarray(inp["w_conv_out"][l], np.float32)
        wpo = np.asarray(inp["w_pool_out"][l], np.float32)
        m12, m4 = [], []
        for c in range(KC):
            cs = slice(c * 128, (c + 1) * 128)
            m12 += [win[:, s_ * D + c * 128:s_ * D + (c + 1) * 128] for s_ in range(4)]
            m4 += [wco[:, cs], wpo[:, cs], win[:, 4 * D + c * 128:4 * D + (c + 1) * 128],
                   win[:, 5 * D + c * 128:5 * D + (c + 1) * 128]]
        common[f"f1gu{l}"] = pack_gu(inp["ffn1_w_gate_up"][l])
        common[f"f1d{l}"] = np.ascontiguousarray(inp["ffn1_w_down"][l], np.float32)
        common[f"m12w{l}"] = np.ascontiguousarray(np.concatenate(m12, axis=1))
        common[f"m4w{l}"] = np.ascontiguousarray(np.concatenate(m4, axis=1))
        common[f"wpg{l}"] = np.ascontiguousarray(inp["w_pool_group"][l], np.float32).reshape(4 * 256, 256)
        common[f"wo{l}"] = np.ascontiguousarray(inp["w_o"][l], np.float32)
        common[f"f2gu{l}"] = pack_gu(inp["ffn2_w_gate_up"][l])
        common[f"f2d{l}"] = np.ascontiguousarray(inp["ffn2_w_down"][l], np.float32)
    maps = []
    for c in range(ncores):
        m = dict(common)
        m["x"] = xs[c]
        maps.append(m)
    return maps


_NC_CACHE = {}


def kernel(**inputs):
    inp = {k: np.asarray(v) for k, v in inputs.items()}
    x = inp["x"].astype(np.float32, copy=False)
    B, S, _ = x.shape
    depth = inp["ffn1_pre"].shape[0]
    nseq = B // NCORES
    xs = [np.ascontiguousarray(x[c * nseq:(c + 1) * nseq].reshape(nseq * S, D)) for c in range(NCORES)]
    key = (nseq, S, depth)
    if key not in _NC_CACHE:
        _NC_CACHE[key] = build_program(nseq, S, depth)
    nc = _NC_CACHE[key]
    res = run_bass_kernel_spmd(nc, make_in_maps(inp, depth, NCORES, xs), core_ids=list(range(NCORES)))
    out = np.stack([r["out"].reshape(nseq, S, D) for r in res.results], axis=0).reshape(B, S, D)
    return out.astype(np.float32, copy=False)
```
